# Optimizing a Trainium2 kernel written in Bass

```python
import math
import jax
import jax.numpy as jnp
from jax import lax
import numpy as np

D_MODEL = 1024
BATCH = 4
SEQ = 8192
DEPTH = 4

MIX_WIDTH = D_MODEL
A_WIDTH = MIX_WIDTH // 2
A_HEADS = 4
A_HEAD_DIM = A_WIDTH // A_HEADS
CHUNK = 128
CONV_K = 5
B_WIDTH = MIX_WIDTH - A_WIDTH
B_HEAD_DIM = 64
B_Q_HEADS = B_WIDTH // B_HEAD_DIM
B_KV_HEADS = 2
Q_PER_KV = B_Q_HEADS // B_KV_HEADS
WINDOW = 128
BLOCK = 128
N_BUCKETS = 32
MAX_DISTANCE = 128
EPS = 1e-6
NEG_INF = -1e30
IN_COLS = (2 * A_WIDTH + A_WIDTH + A_WIDTH + A_WIDTH + 4 * A_HEADS
           + B_WIDTH + 2 * B_KV_HEADS * B_HEAD_DIM + B_WIDTH)

kernel_name = 'hybrid_mlstm_swa_parallel_heads'


def rms_norm(x, w):
    xf = x.astype(jnp.float32)
    y = xf * lax.rsqrt(jnp.mean(xf * xf, axis=-1, keepdims=True) + EPS)
    return (y * w.astype(jnp.float32)).astype(x.dtype)


def centred_depthwise_conv(u, w, b):
    ch = u.shape[-1]
    out = lax.conv_general_dilated(
        u, w[:, None, :].astype(u.dtype), window_strides=(1,),
        padding=[(CONV_K // 2, CONV_K // 2)],
        dimension_numbers=('NWC', 'WIO', 'NWC'), feature_group_count=ch)
    return out + b.astype(u.dtype)


def mlstm_one_direction(q, k, v, i_pre, f_pre):
    bsz, seqlen, nh, dh = q.shape
    nc = seqlen // CHUNK

    def chunks(t):
        return t.reshape(bsz, nc, CHUNK, nh, -1).transpose(1, 0, 3, 2, 4)

    def gchunks(t):
        return t.reshape(bsz, nc, CHUNK, nh).transpose(1, 0, 3, 2)

    qc, kc, vc = chunks(q), chunks(k) * (dh ** -0.5), chunks(v)
    lic = gchunks(i_pre)
    lfc = gchunks(jax.nn.log_sigmoid(f_pre))
    tril = jnp.tril(jnp.ones((CHUNK, CHUNK), dtype=bool))

    def step(carry, inp):
        C, n, m = carry
        qt, kt, vt, li, lf = inp
        b = jnp.cumsum(lf, axis=-1)
        dmat = jnp.where(tril, b[..., :, None] - b[..., None, :] + li[..., None, :], -jnp.inf)
        inter = m[..., None] + b
        m_t = jnp.maximum(inter, jnp.max(dmat, axis=-1))
        s = jnp.einsum('bhtd,bhsd->bhts', qt, kt) * jnp.exp(dmat - m_t[..., None])
        inter_w = jnp.exp(inter - m_t)
        num = (jnp.einsum('bhts,bhsd->bhtd', s, vt)
               + inter_w[..., None] * jnp.einsum('bhtk,bhkv->bhtv', qt, C))
        den = jnp.sum(s, axis=-1) + inter_w * jnp.einsum('bhtk,bhk->bht', qt, n)
        h = num / jnp.maximum(jnp.abs(den), jnp.exp(-m_t))[..., None]
        b_last = b[..., -1]
        w_log = b_last[..., None] - b + li
        m_new = jnp.maximum(m + b_last, jnp.max(w_log, axis=-1))
        decay = jnp.exp(m + b_last - m_new)
        w = jnp.exp(w_log - m_new[..., None])
        C = decay[..., None, None] * C + jnp.einsum('bhs,bhsk,bhsv->bhkv', w, kt, vt)
        n = decay[..., None] * n + jnp.einsum('bhs,bhsk->bhk', w, kt)
        return (C, n, m_new), h

    init = (jnp.zeros((bsz, nh, dh, dh), jnp.float32),
            jnp.zeros((bsz, nh, dh), jnp.float32),
            jnp.zeros((bsz, nh), jnp.float32))
    _, hs = lax.scan(step, init, (qc, kc, vc, lic, lfc))
    return hs.transpose(1, 0, 3, 2, 4).reshape(bsz, seqlen, nh, dh)


def t5_bucket(rel):
    nb = N_BUCKETS // 2
    max_exact = nb // 2
    ret = jnp.where(rel > 0, nb, 0)
    n = jnp.abs(rel)
    nf = jnp.maximum(n, 1).astype(jnp.float32)
    large = max_exact + (jnp.log(nf / max_exact) / math.log(MAX_DISTANCE / max_exact)
                         * (nb - max_exact)).astype(jnp.int32)
    large = jnp.minimum(large, nb - 1)
    return ret + jnp.where(n < max_exact, n, large)


def windowed_gqa(q, k, v, sink, rel_bias):
    bsz, seqlen = q.shape[:2]
    nb = seqlen // BLOCK
    qb = q.reshape(bsz, nb, BLOCK, B_KV_HEADS, Q_PER_KV, B_HEAD_DIM)

    def band(t):
        tb = t.reshape(bsz, nb, BLOCK, B_KV_HEADS, B_HEAD_DIM)
        tp = jnp.pad(tb, ((0, 0), (1, 1), (0, 0), (0, 0), (0, 0)))
        return jnp.concatenate([tp[:, :-2], tp[:, 1:-1], tp[:, 2:]], axis=2)

    kw, vw = band(k), band(v)
    scores = jnp.einsum('bnqhgd,bnkhd->bnhgqk', qb, kw) * (B_HEAD_DIM ** -0.5)
    q_off = jnp.arange(BLOCK)
    k_off = jnp.arange(3 * BLOCK) - BLOCK
    rel = k_off[None, :] - q_off[:, None]
    bias = rel_bias.astype(jnp.float32)[t5_bucket(rel)]
    bias = bias.transpose(2, 0, 1).reshape(B_KV_HEADS, Q_PER_KV, BLOCK, 3 * BLOCK)
    key_pos = jnp.arange(nb)[:, None] * BLOCK + k_off[None, :]
    key_ok = (key_pos >= 0) & (key_pos < seqlen)
    mask = (jnp.abs(rel) <= WINDOW)[None, :, :] & key_ok[:, None, :]
    scores = jnp.where(mask[None, :, None, None], scores + bias, NEG_INF)
    sink_l = sink.astype(jnp.float32).reshape(B_KV_HEADS, Q_PER_KV)[:, :, None]
    m = jnp.maximum(jnp.max(scores, axis=-1), sink_l)
    p = jnp.exp(scores - m[..., None])
    denom = jnp.sum(p, axis=-1) + jnp.exp(sink_l - m)
    out = jnp.einsum('bnhgqk,bnkhd->bnqhgd', p, vw) / denom.transpose(0, 1, 4, 2, 3)[..., None]
    return out.reshape(bsz, seqlen, B_Q_HEADS * B_HEAD_DIM)


def hybrid_layer(x, norm_w, w_in, conv_w, conv_b, gate_b, mhn_w, sink, rel_bias, w_out):
    bsz, seqlen, _ = x.shape
    hn = rms_norm(x, norm_w)
    proj = hn @ w_in.astype(hn.dtype)
    sizes = [2 * A_WIDTH, A_WIDTH, A_WIDTH, A_WIDTH, 4 * A_HEADS,
             B_WIDTH, B_KV_HEADS * B_HEAD_DIM, B_KV_HEADS * B_HEAD_DIM, B_WIDTH]
    cuts = [int(c) for c in np.cumsum(sizes)[:-1]]
    qk_a, v_a, o_a, z_a, g_a, q_b, k_b, v_b, z_b = jnp.split(proj, cuts, axis=-1)

    qk_a = jax.nn.silu(centred_depthwise_conv(qk_a, conv_w, conv_b))
    q_a, k_a = jnp.split(qk_a.astype(jnp.float32), 2, axis=-1)
    hs = (bsz, seqlen, A_HEADS, A_HEAD_DIM)
    q_a, k_a = q_a.reshape(hs), k_a.reshape(hs)
    v_a = v_a.astype(jnp.float32).reshape(hs)
    gates = g_a.astype(jnp.float32) + gate_b.astype(jnp.float32)
    i_f, i_b, f_f, f_b = jnp.split(gates, 4, axis=-1)
    h_fwd = mlstm_one_direction(q_a, k_a, v_a, i_f, f_f)
    h_bwd = jnp.flip(mlstm_one_direction(jnp.flip(q_a, 1), jnp.flip(k_a, 1), jnp.flip(v_a, 1),
                                         jnp.flip(i_b, 1), jnp.flip(f_b, 1)), 1)
    h = jax.nn.sigmoid(o_a.astype(jnp.float32)).reshape(hs) * (h_fwd + h_bwd)
    h = h * lax.rsqrt(jnp.mean(h * h, axis=-1, keepdims=True) + EPS)
    h = h * mhn_w.astype(jnp.float32).reshape(A_HEADS, A_HEAD_DIM)
    y_a = h.reshape(bsz, seqlen, A_WIDTH) * jax.nn.silu(z_a.astype(jnp.float32))

    qh = q_b.astype(jnp.float32).reshape(bsz, seqlen, B_Q_HEADS, B_HEAD_DIM)
    kh = k_b.astype(jnp.float32).reshape(bsz, seqlen, B_KV_HEADS, B_HEAD_DIM)
    vh = v_b.astype(jnp.float32).reshape(bsz, seqlen, B_KV_HEADS, B_HEAD_DIM)
    y_b = windowed_gqa(qh, kh, vh, sink, rel_bias) * jax.nn.silu(z_b.astype(jnp.float32))

    y = jnp.concatenate([y_a, y_b], axis=-1).astype(x.dtype)
    return x + y @ w_out.astype(x.dtype)


def setup_inputs(seed: int = 0) -> dict:
    key = jax.random.key(seed)
    ks = jax.random.split(key, 12)
    f32 = jnp.float32
    x = jax.random.normal(ks[0], (BATCH, SEQ, D_MODEL), f32)
    norm_w = 1.0 + 0.02 * jax.random.normal(ks[1], (DEPTH, D_MODEL), f32)
    w_in = jax.random.normal(ks[2], (DEPTH, D_MODEL, IN_COLS), f32) * (D_MODEL ** -0.5)
    conv_w = jax.random.normal(ks[3], (DEPTH, CONV_K, 2 * A_WIDTH), f32) * (CONV_K ** -0.5)
    conv_b = 0.02 * jax.random.normal(ks[4], (DEPTH, 2 * A_WIDTH), f32)
    ig_b = 0.1 * jax.random.normal(ks[5], (DEPTH, 2 * A_HEADS), f32)
    fg_b = (jnp.tile(jnp.linspace(3.0, 6.0, A_HEADS, dtype=f32), 2)[None, :]
            + 0.1 * jax.random.normal(ks[6], (DEPTH, 2 * A_HEADS), f32))
    gate_b = jnp.concatenate([ig_b, fg_b], axis=-1)
    mhn_w = 1.0 + 0.02 * jax.random.normal(ks[7], (DEPTH, A_WIDTH), f32)
    sink = 0.5 * jax.random.normal(ks[8], (DEPTH, B_Q_HEADS), f32)
    rel_bias = 0.5 * jax.random.normal(ks[9], (N_BUCKETS, B_Q_HEADS), f32)
    w_out = jax.random.normal(ks[10], (DEPTH, MIX_WIDTH, D_MODEL), f32) * (MIX_WIDTH ** -0.5)
    final_norm_w = 1.0 + 0.02 * jax.random.normal(ks[11], (D_MODEL,), f32)
    return {'x': x, 'norm_w': norm_w, 'w_in': w_in, 'conv_w': conv_w, 'conv_b': conv_b,
            'gate_b': gate_b, 'mhn_w': mhn_w, 'sink': sink, 'rel_bias': rel_bias,
            'w_out': w_out, 'final_norm_w': final_norm_w}


def reference(x, norm_w, w_in, conv_w, conv_b, gate_b, mhn_w, sink, rel_bias, w_out, final_norm_w):
    for layer in range(DEPTH):
        x = hybrid_layer(x, norm_w[layer], w_in[layer], conv_w[layer], conv_b[layer],
                         gate_b[layer], mhn_w[layer], sink[layer], rel_bias, w_out[layer])
    return rms_norm(x, final_norm_w)
```

```python
import numpy as np
import ml_dtypes
import concourse.bass as bass
import concourse.mybir as mybir
from concourse.bass_utils import run_bass_kernel_spmd

F32 = mybir.dt.float32
BF16 = mybir.dt.bfloat16
AF = mybir.ActivationFunctionType
ALU = mybir.AluOpType

D = 1024
DEPTH = 4
NH = 4
EPS = 1e-6
LN_SCALE = float(np.log(128.0 ** -0.5))
MASKNEG = -30000.0
NCOL = 3856
C_QA, C_KA, C_QB, C_KB = 0, 512, 1024, 1536
C_VA, C_OA, C_ZA, C_ZB, C_VB, C_G = 1664, 2176, 2688, 3200, 3712, 3840


class Prog:
    ENGS = ("pe", "act", "dve", "pool", "sp")

    def __init__(self, nc, n_dma_sems=14, same_engine_sync=True):
        self.nc = nc
        self.ops = []
        self.epoch = 0
        self.op_epoch = []
        self.n_dma_sems = n_dma_sems
        self.same_engine_sync = same_engine_sync

    PSUM_KEYS = frozenset(["b0", "pj0", "pj1", "cvb", "B4", "B5", "B6", "B7"])

    def op(self, eng, fn, r=(), w=(), dma=False):
        r = tuple(r); w = tuple(w)
        pr = tuple(k for k in r if k in self.PSUM_KEYS)
        if pr:
            r = tuple(k for k in r if k not in self.PSUM_KEYS)
            w = w + pr
        self.ops.append((eng, fn, r, w, dma))
        self.op_epoch.append(self.epoch)

    def emit(self):
        nc = self.nc
        engs = {"pe": nc.tensor, "act": nc.scalar, "dve": nc.vector, "pool": nc.gpsimd, "sp": nc.sync}
        ops = self.ops
        n = len(ops)
        lastw, readers = {}, {}
        deps = [None] * n
        for i, (eng, fn, r, w, dma) in enumerate(ops):
            d = set()
            for k in r:
                j = lastw.get(k)
                if j is not None:
                    d.add(j)
            for k in w:
                j = lastw.get(k)
                if j is not None:
                    d.add(j)
                for j in readers.get(k, ()):
                    d.add(j)
            for k in r:
                readers.setdefault(k, []).append(i)
            for k in w:
                lastw[k] = i
                readers[k] = []
            d.discard(i)
            deps[i] = d
        dma_sem_of, pools, last_on_sem, dma_val, sem_handles = {}, {}, {}, {}, {}
        for i, (eng, fn, r, w, dma) in enumerate(ops):
            if not dma:
                continue
            pk = (eng, "cc") if dma == "cc" else eng
            p = pools.setdefault(pk, [0])
            s = (pk, p[0] % (2 if dma == "cc" else self.n_dma_sems))
            p[0] += 1
            inc = 1 if dma == "cc" else 16
            if s not in sem_handles:
                sem_handles[s] = nc.alloc_semaphore(name=f"d_{eng}_{'cc' if dma == 'cc' else 'q'}_{s[1]}")
            prev = last_on_sem.get(s)
            if prev is not None:
                deps[i].add(prev)
                dma_val[i] = dma_val[prev] + inc
            else:
                dma_val[i] = inc
            last_on_sem[s] = i
            dma_sem_of[i] = s
        marked = [False] * n
        for i in range(n):
            ei = ops[i][0]
            for j in deps[i]:
                if ops[j][4]:
                    continue
                ej = ops[j][0]
                if ej != ei or (self.same_engine_sync and ej != "pe"):
                    marked[j] = True
        ep = self.op_epoch
        esem = {}
        rank = [0] * n
        cnt = {}
        for i in range(n):
            k = (ops[i][0], ep[i])
            if k not in esem:
                esem[k] = nc.alloc_semaphore(name=f"c_{k[0]}_{k[1]}")
                cnt[k] = 0
            if marked[i] and not ops[i][4]:
                cnt[k] += 1
            rank[i] = cnt[k]
        seen = {e: {} for e in self.ENGS}
        nw = 0
        for i, (eng, fn, r, w, dma) in enumerate(ops):
            E = engs[eng]
            need = {}
            for j in deps[i]:
                if ops[j][4]:
                    key = ("d",) + dma_sem_of[j]
                    sem, val = sem_handles[dma_sem_of[j]], dma_val[j]
                else:
                    ej = ops[j][0]
                    if ej == eng and not (self.same_engine_sync and ej != "pe"):
                        continue
                    key = ("c", ej, ep[j])
                    sem, val = esem[(ej, ep[j])], rank[j]
                if need.get(key, (None, 0))[1] < val:
                    need[key] = (sem, val)
            for key, (sem, val) in need.items():
                if seen[eng].get(key, 0) < val:
                    E.wait_ge(sem, val)
                    seen[eng][key] = val
                    nw += 1
            inst = fn()
            if dma == "cc":
                inst.then_inc(sem_handles[dma_sem_of[i]], 1)
            elif dma:
                inst.then_inc(sem_handles[dma_sem_of[i]], 16)
            elif marked[i]:
                inst.then_inc(esem[(eng, ep[i])], 1)
        for s, j in last_on_sem.items():
            key = ("d",) + s
            if seen["sp"].get(key, 0) < dma_val[j]:
                nc.sync.wait_ge(sem_handles[s], dma_val[j])
        return {"ops": n, "waits": nw, "marks": sum(marked), "max_sem": max(cnt.values())}


def build(NT, depth=DEPTH, dbg=False, stop=None):
    nc = bass.Bass("TRN2", target_bir_lowering=False)
    P = Prog(nc)
    T = NT * 128

    def din(name, shape, dt=F32):
        return nc.dram_tensor(name, shape, dt, kind="ExternalInput").ap()

    def dout(name, shape, dt=F32):
        return nc.dram_tensor(name, shape, dt, kind="ExternalOutput").ap()

    def dscr(name, shape, dt=F32):
        return nc.dram_tensor(name, shape, dt, kind="Internal").ap()

    def sb(name, shape, dt=F32):
        return nc.alloc_sbuf_tensor("s_" + name, shape, dt)

    x_d = din("x", [T, D])
    xh_d = din("xh", [128, D])
    win_d = din("w_in", [depth, D, NCOL])
    wout_d = din("w_out", [depth, D, D])
    normw_d = din("normw", [depth, 128, 8])
    convw_d = din("convw", [depth, 128, 40])
    convb_d = din("convb", [depth, 128, 8])
    gateb_d = din("gateb", [depth, 128, 16])
    mhnw_d = din("mhnw", [depth, 128, 512])
    sink_d = din("sink", [depth, 128, 8])
    fnw_d = din("fnw", [128, D])
    msel_d = din("msel", [128, 2])
    bias_d = din("biasT", [4, 128, 1024])
    cst_d = din("consts", [128, 1024])
    xo_d = dout("x_out", [T, D])
    Xs = [dscr("Xs0", [T, D]), dscr("Xs1", [T, D])]
    ccs_in = dscr("ccs_in", [129, 516]); ccs_out = dscr("ccs_out", [258, 516])
    cch_in = dscr("cch_in", [128, D]); cch_out = dscr("cch_out", [256, D])
    RG = [[0, 1], [2, 3], [4, 5], [6, 7]]
    cur = {}
    H1_d = dscr("H1", [NT, 128, 512])
    SO_d = dscr("SO", [NT, 128, 512])
    SZ_d = dscr("SZ", [NT, 128, 512])
    YB_d = dscr("YB", [NT, 128, 512], BF16)
    QT_d = dscr("QT", [NT, 128, 512], BF16)
    KT_d = dscr("KT", [NT, 128, 512], BF16)
    VX_d = dscr("VX", [NT, 128, 512], BF16)
    dbg_outs = {}

    cst = sb("cst", [128, 1024])
    ident = cst[:, 0:128]
    U = {1: cst[:, 128:256], 2: cst[:, 256:384]}
    MN = {1: cst[:, 384:512], 2: cst[:, 512:640]}
    sel = cst[0:4, 640:1152] if False else None
    P.op("sp", lambda: nc.sync.dma_start(out=cst[:], in_=cst_d[:, :]), w=["cst"], dma=True)
    selc = sb("selc", [4, 4 * 128 + 4 + 128])
    seld = din("selc", [4, 4 * 128 + 4 + 128])
    P.op("sp", lambda: nc.sync.dma_start(out=selc[:], in_=seld[:, :]), w=["selc"], dma=True)
    I4 = selc[:, 512:516]
    ones4 = selc[:, 516:644]
    identb = sb("identb", [128, 128], BF16)
    P.op("dve", lambda: nc.vector.tensor_copy(out=identb[:], in_=ident), r=["cst"], w=["identb"])

    win = sb("win", [128, 8, NCOL], BF16)
    wout = sb("wout", [128, 8, D], BF16)
    def load_win(L):
        win_v = win_d[L].rearrange("(kc p) c -> p kc c", p=128)
        for kc in range(8):
            P.op("pool", (lambda kc=kc: nc.gpsimd.dma_start(out=win[:, kc, :], in_=win_v[:, kc, :])), w=[("win", kc)], dma=True)

    def load_wout(L):
        wout_v = wout_d[L].rearrange("(kc p) c -> p kc c", p=128)
        for kc in range(8):
            P.op("pool", (lambda kc=kc: nc.gpsimd.dma_start(out=wout[:, kc, :], in_=wout_v[:, kc, :])), w=[("wout", kc)], dma=True)
    load_win(0)
    load_wout(0)
    msel = sb("msel", [128, 2])
    P.op("sp", lambda: nc.sync.dma_start(out=msel[:], in_=msel_d[:, :]), w=["msel"], dma=True)
    WIN_R = [("win", kc) for kc in range(8)]
    WOUT_R = [("wout", kc) for kc in range(8)]

    normw = sb("normw", [128, 8]); convw = sb("convw", [128, 40]); convb = sb("convb", [128, 8])
    gateb = sb("gateb", [128, 16]); mhnw = sb("mhnw", [128, 512]); sinkt = sb("sinkt", [128, 8])
    esink = sb("esink", [128, 8])
    cdiag = sb("cdiag", [128, 40, 128], BF16)

    def load_params_p1(L):
        for t, d_, nm in [(normw, normw_d, "normw"), (convw, convw_d, "convw"), (convb, convb_d, "convb"),
                          (gateb, gateb_d, "gateb"), (sinkt, sink_d, "sinkt")]:
            P.op("sp", (lambda t=t, d_=d_: nc.sync.dma_start(out=t[:], in_=d_[L, :, :])), w=[nm], dma=True)
        P.op("act", lambda: nc.scalar.activation(out=esink[:], in_=sinkt[:], func=AF.Exp), r=["sinkt"], w=["esink"])
        for j in range(40):
            P.op("dve", (lambda j=j: nc.vector.tensor_scalar(out=cdiag[:, j, :], in0=ident, scalar1=convw[:, j:j + 1], scalar2=None, op0=ALU.mult)),
                 r=["cst", "convw"], w=[("cdiag", j)])

    def load_params_p2(L):
        P.op("sp", lambda: nc.sync.dma_start(out=mhnw[:], in_=mhnw_d[L, :, :]), w=["mhnw"], dma=True)
    load_params_p1(0)
    load_params_p2(0)
    ebias = sb("ebias", [128, 4, 1024])
    btmp = sb("btmp", [128, 1024]); btmp2 = sb("btmp2", [128, 1024])
    for tb in range(4):
        P.op("sp", (lambda tb=tb: nc.sync.dma_start(out=btmp[:], in_=bias_d[tb, :, :])), w=["aex0", "aex1"], dma=True)
        P.op("act", (lambda tb=tb: nc.scalar.activation(out=ebias[:, tb, :], in_=btmp[:], func=AF.Exp)), r=["aex0", "aex1"], w=[("ebias", tb)])

    banks = [nc.alloc_psum_tensor(f"bank{i}", [128, 512], F32) for i in range(8)]
    tpb = banks[0][:, :].bitcast(BF16)
    pj = [banks[1], banks[2]]
    ktp = banks[1][:, 0:256].bitcast(BF16)
    cvb = banks[3]
    gps = banks[4]
    Xs4 = banks[4][:, 256:268]
    tps = banks[4][:, 268:284]; dec_ps = banks[4][:, 284:288]
    cmb = banks[5]
    stb = banks[6]
    itb = banks[7]
    o_ps = banks[7][:, 0:260]

    NS = 2
    xt = [sb(f"xt{s}", [128, D]) for s in range(2)]
    junk = hn_junk = None
    ss = sb("ss", [128, 1]); lnv = sb("lnv", [128, 1]); rstd = sb("rstd", [128, 1])
    hn = sb("hn", [128, D], BF16)
    HN = [("hn", 0), ("hn", 1)]
    hnT = sb("hnT", [128, 8, 128], BF16)
    raw = [sb(f"raw{s}", [128, 8, 132], BF16) for s in range(NS)]
    qbT = [sb(f"qbT{s}", [128, 4, 128], BF16) for s in range(NS)]
    kbT = [[sb(f"kbT{s}_{g}", [128, 128], BF16) for g in range(2)] for s in range(4)]
    vbx = [sb(f"vbx{s}", [128, 2, 65], BF16) for s in range(4)]
    vext = [sb(f"vext{s}", [128, 4, 128], BF16) for s in range(NS)]
    szb = [sb(f"szb{s}", [128, 512]) for s in range(NS)]
    so_t = [sb(f"so{s}", [128, 512]) for s in range(2)]
    sz_t = [sb(f"sz{s}", [128, 512]) for s in range(2)]
    gall = sb("gall", [128, NT + 1, 16])
    qT = [sb(f"qT{s}", [128, 4, 128], BF16) for s in range(2)]
    kT = [sb(f"kT{s}", [128, 4, 128], BF16) for s in range(2)]
    onesb = sb("onesb", [128, 1], BF16)
    P.op("dve", lambda: nc.vector.memset(onesb[:], 1.0), w=["onesb"])
    for s in range(NS):
        P.op("dve", (lambda s=s: nc.vector.memset(raw[s][:], 0.0)), w=[f"raw{s}"])
    for s in range(4):
        P.op("dve", (lambda s=s: nc.vector.memset(vbx[s][:], 1.0)), w=[f"vbx{s}"])
        for g in range(2):
            P.op("dve", (lambda s=s, g=g: nc.vector.memset(kbT[s][g][:], 0.0)), w=[f"kbT{s}"])
    stt = sb("stt", [128, 516]); stb16 = sb("stb16", [128, 516], BF16)
    Cst = stt[:, 0:512].rearrange("p (h d) -> p h d", h=4); nst = stt[:, 512:516]
    Cb = stb16[:, 0:512].rearrange("p (h d) -> p h d", h=4); nb = stb16[:, 512:516]
    st2 = sb("st2", [128, 516])
    mst = [sb(f"mst{s}", [NH, 1]) for s in range(2)]
    ge = sb("ge", [128, 4]); gsp = sb("gsp", [128, 4])
    gsb = sb("gsb", [NH, 2, 128]); gg = sb("gg", [NH, 128]); Rg = sb("Rg", [NH, 4, 128])
    NG = sb("NG", [NH, 128]); dl = sb("dl", [NH, 1]); dd = sb("dd", [NH, 4]); glast = sb("glast", [NH, 1])
    tok = sb("tok", [128, 16]); decay = sb("decay", [128, 4])
    arg = [sb(f"arg{h}", [128, 128]) for h in range(NH)]
    pT = [sb(f"pT{h}", [128, 128], BF16) for h in range(NH)]
    itmp = sb("itmp", [128, 512]); iden = sb("iden", [128, 4])
    hu = sb("hu", [128, 512]); dn = sb("dn", [128, 4]); rdn = sb("rdn", [128, 4])
    hdir = [sb(f"hdir{s}", [128, 512]) for s in range(2)]
    kw = sb("kw", [128, NH, 128], BF16)
    apT = sb("apT", [128, 2, 3, 512], BF16)
    aex = [btmp[:, 0:512], btmp[:, 512:1024]]
    aden = sb("aden", [128, 4]); arden = sb("arden", [128, 4]); atmp = sb("atmp", [128, 4, 64])
    yb = [sb(f"yb{s}", [128, 512], BF16) for s in range(2)]
    tht = sb("tht", [128, 512])

    def dbg_out(name, ap_sb, shape, keys, dt=F32):
        if not dbg:
            return
        o = dout("dbg_" + name, shape, dt)
        dbg_outs[name] = o
        P.op("sp", lambda: nc.sync.dma_start(out=o, in_=ap_sb), r=keys, dma=True)

    def load_x1(i):
        xs = i % 2
        L = cur["L"]
        if i == NT and L > 0:
            P.op("sp", lambda: nc.sync.dma_start(out=xt[xs][:], in_=cch_out[0:128, :]), r=["cch_out"], w=[f"xt{xs}"], dma=True)
            P.op("sp", lambda: nc.sync.dma_start(out=btmp[:], in_=cch_out[128:256, :]), r=["cch_out"], w=["aex0", "aex1"], dma=True)
            P.op("dve", lambda: nc.vector.tensor_scalar(out=btmp[:], in0=btmp[:], scalar1=msel[:, 1:2], scalar2=None, op0=ALU.mult), r=["aex0", "aex1", "msel"], w=["aex0", "aex1"])
            P.op("dve", lambda: nc.vector.scalar_tensor_tensor(out=xt[xs][:], in0=xt[xs][:], scalar=msel[:, 0:1], in1=btmp[:], op0=ALU.mult, op1=ALU.add), r=[f"xt{xs}", "msel", "aex0", "aex1"], w=[f"xt{xs}"])
        else:
            src = xh_d[:, :] if i == NT else cur["xin"][i * 128:(i + 1) * 128, :]
            xkey = [] if (i == NT or L == 0) else [("X", L % 2, i)]
            P.op("sp", lambda: nc.sync.dma_start(out=xt[xs][:], in_=src), r=xkey, w=[f"xt{xs}"], dma=True)

    def stageA1n(i, act_head=None):
        xs = i % 2
        P.op("act", lambda: nc.scalar.activation(out=hn[:], in_=xt[xs][:], func=AF.Square, accum_out=ss[:]), r=[f"xt{xs}"], w=HN + ["ss"])
        if act_head is not None:
            act_head()
        P.op("act", lambda: nc.scalar.activation(out=lnv[:], in_=ss[:], func=AF.Ln, scale=1.0 / D, bias=epsb[:]), r=["ss", "epsb"], w=["lnv"])
        P.op("act", lambda: nc.scalar.activation(out=rstd[:], in_=lnv[:], func=AF.Exp, scale=-0.5), r=["lnv"], w=["rstd"])
        P.op("dve", lambda: nc.vector.tensor_scalar(out=hn[:], in0=xt[xs][:], scalar1=rstd[:], scalar2=None, op0=ALU.mult), r=[f"xt{xs}", "rstd"], w=HN)

    def stageA1p():
        for kc in range(8):
            P.op("pe", (lambda kc=kc: nc.tensor.transpose(out=tpb[:, kc * 128:(kc + 1) * 128], in_=hn[:, kc * 128:(kc + 1) * 128], identity=identb[:])),
                 r=HN + ["identb"], w=["b0"])
        P.op("dve", lambda: nc.vector.tensor_tensor(out=hnT[:], in0=tpb.rearrange("p (k t) -> p k t", k=8),
                                                    in1=normw[:].unsqueeze(2).broadcast_to([128, 8, 128]), op=ALU.mult),
             r=["b0", "normw"], w=["hnT"])

    def stageA1f(i):
        xs = i % 2; s3 = i % NS; s4 = i % 4
        groups = [(0, 4), (4, 8), (8, 12), (12, 13)]
        for gi, (c0, c1) in enumerate(groups):
            bk = pj[gi % 2]; bkey = f"pj{gi % 2}"
            for c in range(c0, c1):
                for kc in range(8):
                    P.op("pe", (lambda c=c, kc=kc, bk=bk, c0=c0: nc.tensor.matmul(bk[:, (c - c0) * 128:(c - c0 + 1) * 128], lhsT=win[:, kc, c * 128:(c + 1) * 128],
                                                                              rhs=hnT[:, kc, :], start=(kc == 0), stop=(kc == 7))),
                         r=["hnT", ("win", kc)], w=[bkey])
            if gi < 2:
                P.op("act", (lambda bk=bk, c0=c0: nc.scalar.copy(out=raw[s3][:, c0:c0 + 4, 2:130], in_=bk[:, :].rearrange("p (c t) -> p c t", c=4))),
                     r=[bkey], w=[f"raw{s3}"])
            elif gi == 2:
                P.op("act", (lambda bk=bk: nc.scalar.activation(out=qbT[s3][:], in_=bk[:, :].rearrange("p (c t) -> p c t", c=4), func=AF.Copy, scale=0.125)),
                     r=[bkey], w=[f"qbT{s3}"])
            else:
                for g in range(2):
                    P.op("act", (lambda bk=bk, g=g: nc.scalar.copy(out=kbT[s4][g][64 * g:64 * g + 64, :], in_=bk[64 * g:64 * g + 64, 0:128])), r=[bkey], w=[f"kbT{s4}"])
    def stageA2(i, part=None):
        xs = i % 2; s3 = i % NS; s4 = i % 4
        tg = [(C_VA, 512, "va"), (C_OA, 512, "oa"), (C_ZA, 512, "za"), (C_ZB, 512, "zb"), (C_VB, 144, "vbg")]
        for gi, (cs, wd, nm) in enumerate(tg):
            if part == "a" and nm == "zb":
                continue
            if part == "b" and nm != "zb":
                continue
            if i == NT and nm in ("oa", "za", "zb", "va"):
                continue
            bk = pj[gi % 2]; bkey = f"pj{gi % 2}"
            for kc in range(8):
                P.op("pe", (lambda kc=kc, bk=bk, cs=cs, wd=wd: nc.tensor.matmul(bk[:, 0:wd], lhsT=hnT[:, kc, :], rhs=win[:, kc, cs:cs + wd], start=(kc == 0), stop=(kc == 7))),
                     r=["hnT", ("win", kc)], w=[bkey])
            if nm == "va":
                P.op("dve", (lambda bk=bk: nc.vector.tensor_copy(out=vext[s3][:], in_=bk[:, :].rearrange("p (h d) -> p h d", h=4))), r=[bkey], w=[f"vext{s3}"])
            elif nm == "oa":
                P.op("act", (lambda bk=bk: nc.scalar.activation(out=so_t[xs][:], in_=bk[:, :], func=AF.Tanh, scale=0.5)), r=[bkey], w=[f"so{xs}"])
                P.op("sp", lambda: nc.sync.dma_start(out=SO_d[i, :, :], in_=so_t[xs][:]), r=[f"so{xs}"], w=[("SO", i)], dma=True)
            elif nm == "za":
                P.op("act", (lambda bk=bk: nc.scalar.activation(out=tht[:], in_=bk[:, :], func=AF.Tanh, scale=0.5)), r=[bkey], w=["tht"])
                P.op("dve", (lambda bk=bk: nc.vector.scalar_tensor_tensor(out=sz_t[xs][:], in0=tht[:], scalar=1.0, in1=bk[:, :], op0=ALU.add, op1=ALU.mult)), r=["tht", bkey], w=[f"sz{xs}"])
                P.op("sp", lambda: nc.sync.dma_start(out=SZ_d[i, :, :], in_=sz_t[xs][:]), r=[f"sz{xs}"], w=[("SZ", i)], dma=True)
            elif nm == "zb":
                P.op("act", (lambda bk=bk: nc.scalar.activation(out=tht[:], in_=bk[:, :], func=AF.Tanh, scale=0.5)), r=[bkey], w=["tht"])
                P.op("dve", (lambda bk=bk: nc.vector.scalar_tensor_tensor(out=szb[s3][:], in0=tht[:], scalar=1.0, in1=bk[:, :], op0=ALU.add, op1=ALU.mult)), r=["tht", bkey], w=[f"szb{s3}"])
            else:
                P.op("dve", (lambda bk=bk: nc.vector.tensor_copy(out=vbx[s4][:, :, 0:64], in_=bk[:, 0:128].rearrange("p (g d) -> p g d", g=2))), r=[bkey], w=[f"vbx{s4}"])
                P.op("dve", (lambda bk=bk: nc.vector.tensor_tensor(out=gall[:, i, :], in0=bk[:, 128:144], in1=gateb[:], op=ALU.add)), r=[bkey, "gateb"], w=[("gall", i)])

    epsb = sb("epsb", [128, 1])
    P.op("dve", lambda: nc.vector.memset(epsb[:], EPS), w=["epsb"])
    eps4b = sb("eps4b", [128, 1])
    P.op("dve", lambda: nc.vector.memset(eps4b[:], 4.0 * EPS), w=["eps4b"])
    lnsb = sb("lnsb", [NH, 1])
    P.op("dve", lambda: nc.vector.memset(lnsb[:], LN_SCALE), w=["lnsb"])

    def conv(i):
        s3 = i % NS; sn = (i + 1) % NS; sq = i % 2
        if i == 0:
            P.op("dve", lambda: nc.vector.memset(raw[s3][:, :, 0:2], 0.0), w=[f"raw{s3}"])
        if i == NT - 1:
            P.op("dve", lambda: nc.vector.tensor_copy(out=raw[s3][:, :, 130:131], in_=raw[sn][:, :, 129:130]), r=[f"raw{sn}"], w=[f"raw{s3}"])
            P.op("dve", lambda: nc.vector.tensor_copy(out=raw[s3][:, :, 131:132], in_=raw[sn][:, :, 128:129]), r=[f"raw{sn}"], w=[f"raw{s3}"])
        else:
            P.op("dve", lambda: nc.vector.tensor_copy(out=raw[s3][:, :, 130:132], in_=raw[sn][:, :, 2:4]), r=[f"raw{sn}"], w=[f"raw{s3}"])
            P.op("dve", lambda: nc.vector.tensor_copy(out=raw[sn][:, :, 0:2], in_=raw[s3][:, :, 128:130]), r=[f"raw{s3}"], w=[f"raw{sn}"])
        for half in range(2):
            cbank, ckey = (cvb, "cvb") if half == 0 else (stb, "B6")
            for c4 in range(4):
                cc = half * 4 + c4
                for k in range(5):
                    P.op("pe", (lambda cc=cc, c4=c4, k=k, cbank=cbank: nc.tensor.matmul(cbank[:, c4 * 128:(c4 + 1) * 128], lhsT=cdiag[:, cc * 5 + k, :], rhs=raw[s3][:, cc, k:k + 128],
                                                                                     start=(k == 0), stop=(k == 4))),
                         r=[f"raw{s3}", ("cdiag", cc * 5 + k)], w=[ckey])
        for half in (1, 0):
            cbank, ckey = (cvb, "cvb") if half == 0 else (stb, "B6")
            dst = qT[sq] if half == 0 else kT[sq]
            dkey = f"qT{sq}" if half == 0 else f"kT{sq}"
            for c4 in range(4):
                cc = half * 4 + c4
                P.op("act", (lambda cc=cc, c4=c4, dst=dst, cbank=cbank: nc.scalar.activation(out=dst[:, c4, :], in_=cbank[:, c4 * 128:(c4 + 1) * 128], func=AF.Silu, bias=convb[:, cc:cc + 1])),
                     r=[ckey, "convb"], w=[dkey])

    gsp2 = sb("gsp2", [128, NT, 4])

    def mlstm_G_head(i):
        P.op("act", lambda: nc.scalar.activation(out=ge[:], in_=gall[:, i, 4:8], func=AF.Exp, scale=-1.0), r=[("gall", i)], w=["ge"])
        P.op("act", lambda: nc.scalar.activation(out=gsp[:], in_=ge[:], func=AF.Ln, bias=1.0), r=["ge"], w=["gsp"])

    def softplus_all_dir2():
        GALL = [("gall", i) for i in range(NT)]
        P.op("act", lambda: nc.scalar.activation(out=gsp2[:], in_=gall[:, 0:NT, 12:16], func=AF.Exp, scale=-1.0), r=GALL, w=["gsp2"])
        P.op("act", lambda: nc.scalar.activation(out=gsp2[:], in_=gsp2[:], func=AF.Ln, bias=1.0), r=["gsp2"], w=["gsp2"])

    def mlstm_G(i, d, head_done=False):
        fwd = (d == 1)
        gc = 8 * (d - 1)
        last = 127 if fwd else 0
        mp = mst[i % 2]; mn = mst[(i + 1) % 2]
        mpk = f"mst{i % 2}"; mnk = f"mst{(i + 1) % 2}"
        if d == 1:
            spv = gsp[:]; spk = "gsp"
            if not head_done:
                mlstm_G_head(i)
        else:
            spv = gsp2[:, i, :]; spk = "gsp2"
        P.op("pe", lambda: nc.tensor.matmul(gps[0:4, 0:128], lhsT=gall[:, i, gc:gc + 4], rhs=ident, start=True, stop=False), r=[("gall", i), "cst"], w=["B4"])
        P.op("pe", lambda: nc.tensor.matmul(gps[0:4, 0:128], lhsT=spv, rhs=U[d], start=False, stop=True), r=[spk, "cst"], w=["B4"])
        P.op("pe", lambda: nc.tensor.matmul(gps[0:4, 128:256], lhsT=spv, rhs=U[d], start=True, stop=True), r=[spk, "cst"], w=["B4"])
        P.op("dve", lambda: nc.vector.tensor_copy(out=gsb[:], in_=gps[0:4, 0:256].rearrange("p (a t) -> p a t", a=2)), r=["B4"], w=["gsb"])
        a_ap = gsb[:, 0, :]; nb_ap = gsb[:, 1, :]
        if fwd:
            P.op("dve", lambda: nc.vector.tensor_tensor_scan(out=gg[:], data0=ones4, data1=a_ap, initial=mp[:], op0=ALU.mult, op1=ALU.max), r=["gsb", "selc", mpk], w=["gg"])
        else:
            P.op("dve", lambda: nc.vector.tensor_tensor_scan(out=gg[:, ::-1], data0=ones4, data1=gsb[:, 0, ::-1], initial=mp[:], op0=ALU.mult, op1=ALU.max), r=["gsb", "selc", mpk], w=["gg"])
        P.op("dve", lambda: nc.vector.tensor_copy(out=glast[:], in_=gg[:, last:last + 1]), r=["gg"], w=["glast"])
        P.op("dve", lambda: nc.vector.tensor_tensor(out=mn[:], in0=gg[:, last:last + 1], in1=gsb[:, 1, last:last + 1], op=ALU.subtract), r=["gg", "gsb"], w=[mnk])
        P.op("dve", lambda: nc.vector.tensor_scalar(out=Rg[:, 0, :], in0=a_ap, scalar1=lnsb[:], scalar2=None, op0=ALU.add), r=["gsb", "lnsb"], w=["Rg0"])
        P.op("dve", lambda: nc.vector.tensor_scalar(out=Rg[:, 1, :], in0=gg[:], scalar1=-1.0, scalar2=mp[:], op0=ALU.mult, op1=ALU.add), r=["gg", mpk], w=["Rg1"])
        P.op("dve", lambda: nc.vector.tensor_tensor(out=Rg[:, 2, :], in0=nb_ap, in1=gg[:], op=ALU.subtract), r=["gsb", "gg"], w=["Rg2"])
        P.op("dve", lambda: nc.vector.tensor_scalar(out=Rg[:, 3, :], in0=a_ap, scalar1=glast[:], scalar2=lnsb[:], op0=ALU.subtract, op1=ALU.add), r=["gsb", "glast", "lnsb"], w=["Rg3"])
        P.op("dve", lambda: nc.vector.tensor_scalar(out=NG[:], in0=gg[:], scalar1=-1.0, scalar2=None, op0=ALU.mult), r=["gg"], w=["NG"])
        P.op("dve", lambda: nc.vector.tensor_copy(out=dl[:], in_=Rg[:, 1, last:last + 1]), r=["Rg1"], w=["dl"])
        P.op("dve", lambda: nc.vector.tensor_scalar(out=dd[:], in0=I4, scalar1=dl[:], scalar2=None, op0=ALU.mult), r=["dl", "selc"], w=["dd"])
        P.op("act", lambda: nc.scalar.activation(out=Rg[:, 1:4, :], in_=Rg[:, 1:4, :], func=AF.Exp), r=["Rg1", "Rg2", "Rg3"], w=["Rg1", "Rg2", "Rg3"])

    def mlstm_Gb():
        for h in range(NH):
            P.op("pe", (lambda h=h: nc.tensor.matmul(cmb[:, h * 128:(h + 1) * 128], lhsT=selc[:, h * 128:(h + 1) * 128], rhs=NG[:], start=True, stop=True)), r=["NG", "selc"], w=["B5"])
        P.op("pe", lambda: nc.tensor.matmul(dec_ps, lhsT=ones4, rhs=dd[:], start=True, stop=True), r=["dd", "selc"], w=["B4"])
        for q in range(4):
            P.op("pe", (lambda q=q: nc.tensor.transpose(out=tps[:, q * 4:(q + 1) * 4], in_=Rg[:, q, :], identity=I4)), r=[f"Rg{q}", "selc"], w=["B4"])
        P.op("dve", lambda: nc.vector.tensor_copy(out=tok[:], in_=tps), r=["B4"], w=["tok"])
        P.op("act", lambda: nc.scalar.activation(out=decay[:], in_=dec_ps, func=AF.Exp), r=["B4"], w=["decay"])

    def mlstm_K(sq):
        for h in range(NH):
            P.op("pe", (lambda h=h: nc.tensor.transpose(out=ktp[:, h * 128:(h + 1) * 128], in_=kT[sq][:, h, :], identity=identb[:])), r=[f"kT{sq}", "identb"], w=["pj0"])
        P.op("dve", lambda: nc.vector.tensor_tensor(out=kw[:], in0=ktp.rearrange("p (h d) -> p h d", h=4), in1=tok[:, 12:16].unsqueeze(2).broadcast_to([128, 4, 128]), op=ALU.mult),
             r=["pj0", "tok"], w=["kw"])

    def mlstm_H1(d, sq, sv, vkey):
        for h in range(NH):
            P.op("pe", (lambda h=h: nc.tensor.matmul(stb[:, h * 128:(h + 1) * 128], lhsT=kT[sq][:, h, :], rhs=qT[sq][:, h, :], start=True, stop=True)), r=[f"kT{sq}", f"qT{sq}"], w=["B6"])
        for h in range(NH):
            P.op("pe", (lambda h=h: nc.tensor.matmul(itb[:, h * 128:(h + 1) * 128], lhsT=qT[sq][:, h, :], rhs=Cb[:, h, :], start=True, stop=True)), r=[f"qT{sq}", "Cb"], w=["B7"])
            P.op("pe", (lambda h=h: nc.tensor.matmul(Xs4[:, h:h + 1], lhsT=qT[sq][:, h, :], rhs=nb[:, h:h + 1], start=True, stop=True)), r=[f"qT{sq}", "Cb"], w=["B4"])
        for h in range(NH):
            P.op("pe", (lambda h=h: nc.tensor.matmul(cvb[:, h * 128:(h + 1) * 128], lhsT=kw[:, h, :], rhs=sv[:, h, :], start=True, stop=True)), r=["kw", vkey], w=["cvb"])
            P.op("pe", (lambda h=h: nc.tensor.matmul(Xs4[:, 4 + h:5 + h], lhsT=kw[:, h, :], rhs=onesb[:], start=True, stop=True)), r=["kw", "onesb"], w=["B4"])
        for h in range(NH):
            P.op("dve", (lambda h=h: nc.vector.scalar_tensor_tensor(out=arg[h][:], in0=cmb[:, h * 128:(h + 1) * 128], scalar=tok[:, h:h + 1], in1=MN[d], op0=ALU.add, op1=ALU.add)),
                 r=["B5", "tok", "cst"], w=[f"arg{h}"])
            P.op("act", (lambda h=h: nc.scalar.activation(out=arg[h][:], in_=arg[h][:], func=AF.Exp)), r=[f"arg{h}"], w=[f"arg{h}"])
        for h in range(NH):
            P.op("dve", (lambda h=h: nc.vector.tensor_tensor(out=pT[h][:], in0=arg[h][:], in1=stb[:, h * 128:(h + 1) * 128], op=ALU.mult)), r=[f"arg{h}", "B6"], w=[f"pT{h}"])
        P.op("dve", lambda: nc.vector.tensor_tensor(out=itmp[:].rearrange("p (h d) -> p h d", h=4), in0=itb[:, :].rearrange("p (h d) -> p h d", h=4),
                                                    in1=tok[:, 4:8].unsqueeze(2).broadcast_to([128, 4, 128]), op=ALU.mult), r=["B7", "tok"], w=["itmp"])
        P.op("dve", lambda: nc.vector.tensor_tensor(out=iden[:], in0=Xs4[:, 0:4], in1=tok[:, 4:8], op=ALU.mult), r=["B4", "tok"], w=["iden"])

    def mlstm_H1b():
        for h in range(NH):
            P.op("dve", (lambda h=h: nc.vector.scalar_tensor_tensor(out=Cst[:, h, :], in0=Cst[:, h, :], scalar=decay[:, h:h + 1], in1=cvb[:, h * 128:(h + 1) * 128], op0=ALU.mult, op1=ALU.add)),
                 r=["Cst", "decay", "cvb"], w=["Cst"])
        P.op("dve", lambda: nc.vector.tensor_tensor(out=nst, in0=nst, in1=decay[:], op=ALU.mult), r=["Cst", "decay"], w=["Cst"])
        P.op("dve", lambda: nc.vector.tensor_tensor(out=nst, in0=nst, in1=Xs4[:, 4:8], op=ALU.add), r=["Cst", "B4"], w=["Cst"])

    def mlstm_H2(sv, vkey, hout, hkey):
        for h in range(NH):
            P.op("pe", (lambda h=h: nc.tensor.matmul(stb[:, h * 128:(h + 1) * 128], lhsT=pT[h][:], rhs=sv[:, h, :], start=True, stop=True)), r=[f"pT{h}", vkey], w=["B6"])
            P.op("pe", (lambda h=h: nc.tensor.matmul(Xs4[:, 8 + h:9 + h], lhsT=pT[h][:], rhs=onesb[:], start=True, stop=True)), r=[f"pT{h}", "onesb"], w=["B4"])
        P.op("dve", lambda: nc.vector.tensor_tensor(out=hu[:], in0=itmp[:], in1=stb[:, :], op=ALU.add), r=["itmp", "B6"], w=["hu"])
        P.op("dve", lambda: nc.vector.tensor_tensor(out=iden[:], in0=iden[:], in1=Xs4[:, 8:12], op=ALU.add), r=["iden", "B4"], w=["iden"])
        P.op("dve", lambda: nc.vector.tensor_tensor(out=dn[:], in0=iden[:], in1=tok[:, 8:12], op=ALU.max), r=["iden", "tok"], w=["dn"])
        P.op("dve", lambda: nc.vector.scalar_tensor_tensor(out=dn[:], in0=iden[:], scalar=-1.0, in1=dn[:], op0=ALU.mult, op1=ALU.max), r=["iden", "dn"], w=["dn"])
        P.op("dve", lambda: nc.vector.reciprocal(out=rdn[:], in_=dn[:]), r=["dn"], w=["rdn"])
        P.op("dve", lambda: nc.vector.tensor_tensor(out=hout[:, :].rearrange("p (h d) -> p h d", h=4), in0=hu[:].rearrange("p (h d) -> p h d", h=4),
                                                    in1=rdn[:].unsqueeze(2).broadcast_to([128, 4, 128]), op=ALU.mult), r=["hu", "rdn"], w=[hkey])

    def mlstm_cast():
        P.op("act", lambda: nc.scalar.copy(out=stb16[:], in_=stt[:]), r=["Cst"], w=["Cb"])

    def att_blocks(i):
        blocks = []
        if i > 0:
            blocks.append(((i - 1) % 4, 0))
        blocks.append((i % 4, 1))
        blocks.append(((i + 1) % 4, 3 if i == NT - 1 else 2))
        return blocks

    sbanks = [(pj[0], "pj0"), (pj[1], "pj1"), (banks[0], "b0"), (banks[5], "B5")]

    def attention_S(i):
        s3 = i % NS
        blocks = att_blocks(i)
        n = 0
        for g in range(2):
            for bi, (ks, tb) in enumerate(blocks):
                bk, bkey = sbanks[n % 4]; ax = n % 2
                n += 1
                P.op("pe", (lambda bk=bk, ks=ks, g=g: nc.tensor.matmul(bk[:, :], lhsT=kbT[ks][g][:], rhs=qbT[s3][:], start=True, stop=True)), r=[f"kbT{ks}", f"qbT{s3}"], w=[bkey])
                P.op("act", (lambda bk=bk, ax=ax: nc.scalar.activation(out=aex[ax], in_=bk[:, :], func=AF.Exp)), r=[bkey], w=[f"aex{ax}"])
                P.op("pool", (lambda bi=bi, tb=tb, g=g, ax=ax: nc.gpsimd.tensor_tensor(out=apT[:, g, bi, :], in0=aex[ax], in1=ebias[:, tb, g * 512:(g + 1) * 512], op=ALU.mult)),
                     r=[f"aex{ax}", ("ebias", tb)], w=[("apT", g, bi)])

    def attention_O(i):
        s3 = i % NS; ys = i % 2
        blocks = att_blocks(i)
        nb = len(blocks)

        def group_pe(g):
            ob = (o_ps, "B7") if g == 0 else (cvb[:, 0:260], "cvb")
            for c in range(4):
                for bi, (ks, tb) in enumerate(blocks):
                    P.op("pe", (lambda c=c, bi=bi, ks=ks: nc.tensor.matmul(ob[0][:, c * 65:(c + 1) * 65], lhsT=apT[:, g, bi, c * 128:(c + 1) * 128], rhs=vbx[ks][:, g, :],
                                                                        start=(bi == 0), stop=(bi == nb - 1))),
                         r=[("apT", g, bi), f"vbx{ks}"], w=[ob[1]])

        def group(g):
            ob = (o_ps, "B7") if g == 0 else (cvb[:, 0:260], "cvb")
            o3 = ob[0].rearrange("p (c e) -> p c e", c=4)
            P.op("dve", lambda: nc.vector.tensor_tensor(out=aden[:], in0=o3[:, :, 64], in1=esink[:, 4 * g:4 * g + 4], op=ALU.add), r=[ob[1], "esink"], w=["aden"])
            P.op("dve", lambda: nc.vector.reciprocal(out=arden[:], in_=aden[:]), r=["aden"], w=["arden"])
            P.op("dve", lambda: nc.vector.tensor_tensor(out=atmp[:], in0=o3[:, :, 0:64], in1=arden[:].unsqueeze(2).broadcast_to([128, 4, 64]), op=ALU.mult), r=[ob[1], "arden"], w=["atmp"])
            P.op("dve", lambda: nc.vector.scalar_tensor_tensor(out=yb[ys][:, g * 256:(g + 1) * 256], in0=atmp[:].rearrange("p c e -> p (c e)"), scalar=0.5, in1=szb[s3][:, g * 256:(g + 1) * 256], op0=ALU.mult, op1=ALU.mult),
                 r=["atmp", f"szb{s3}"], w=[f"yb{ys}"])
        group_pe(0)
        group_pe(1)
        group(0)
        group(1)
        P.op("sp", lambda: nc.sync.dma_start(out=YB_d[i, :, :], in_=yb[ys][:]), r=[f"yb{ys}"], w=[("YB", i)], dma=True)

    def pass1_tile(i):
        sq = i % 2; s3 = i % NS
        if i + 2 <= NT:
            load_x1(i + 2)
        mlstm_G(i, 1, head_done=True)
        stageA1f(i + 1)
        mlstm_Gb()
        conv(i)
        P.op("sp", lambda: nc.sync.dma_start(out=QT_d[i, :, :], in_=qT[sq][:].rearrange("p h t -> p (h t)")), r=[f"qT{sq}"], w=[("QT", i)], dma=True)
        P.op("sp", lambda: nc.sync.dma_start(out=KT_d[i, :, :], in_=kT[sq][:].rearrange("p h t -> p (h t)")), r=[f"kT{sq}"], w=[("KT", i)], dma=True)
        P.op("sp", lambda: nc.sync.dma_start(out=VX_d[i, :, :], in_=vext[s3][:].rearrange("p h t -> p (h t)")), r=[f"vext{s3}"], w=[("VX", i)], dma=True)
        stageA2(i + 1, "a")
        mlstm_K(sq)
        mlstm_H1(1, sq, vext[s3], f"vext{s3}")
        if i + 2 <= NT:
            stageA1n(i + 2, act_head=lambda: mlstm_G_head(i + 1))
        elif i + 1 <= NT - 1:
            mlstm_G_head(i + 1)
        mlstm_H1b()
        stageA2(i + 1, "b")
        attention_S(i)
        mlstm_H2(vext[s3], f"vext{s3}", hdir[sq], f"hdir{sq}")
        if i + 2 <= NT:
            stageA1p()
        P.op("sp", lambda: nc.sync.dma_start(out=H1_d[i, :, :], in_=hdir[sq][:]), r=[f"hdir{sq}"], w=[("H1", i)], dma=True)
        attention_O(i)
        mlstm_cast()

    h1t = [sb(f"h1t{s}", [128, 512]) for s in range(2)]
    ybt = [sb(f"ybt{s}", [128, 512], BF16) for s in range(2)]
    vx2 = [sb(f"vx2{s}", [128, 4, 128], BF16) for s in range(2)]
    hs = sb("hs", [128, 512]); hsq = sb("hsq", [128, 128]); hss = sb("hss", [128, 4]); hl = sb("hl", [128, 4]); hr = sb("hr", [128, 4])
    mz = sb("mz", [128, 512])
    ycat = hn; yT = hnT
    xo = xt
    fnw = btmp2
    mtmp = sb("mtmp", [NH, 1])

    def load2(i):
        s = i % 2
        P.op("sp", lambda: nc.sync.dma_start(out=qT[s][:].rearrange("p h t -> p (h t)"), in_=QT_d[i, :, :]), r=[("QT", i)], w=[f"qT{s}"], dma=True)
        P.op("sp", lambda: nc.sync.dma_start(out=kT[s][:].rearrange("p h t -> p (h t)"), in_=KT_d[i, :, :]), r=[("KT", i)], w=[f"kT{s}"], dma=True)
        P.op("sp", lambda: nc.sync.dma_start(out=vx2[s][:].rearrange("p h t -> p (h t)"), in_=VX_d[i, :, :]), r=[("VX", i)], w=[f"vx2{s}"], dma=True)
        P.op("sp", lambda: nc.sync.dma_start(out=h1t[s][:], in_=H1_d[i, :, :]), r=[("H1", i)], w=[f"h1t{s}"], dma=True)
        P.op("sp", lambda: nc.sync.dma_start(out=so_t[s][:], in_=SO_d[i, :, :]), r=[("SO", i)], w=[f"so{s}"], dma=True)
        P.op("sp", lambda: nc.sync.dma_start(out=sz_t[s][:], in_=SZ_d[i, :, :]), r=[("SZ", i)], w=[f"sz{s}"], dma=True)
        P.op("sp", lambda: nc.sync.dma_start(out=ybt[s][:], in_=YB_d[i, :, :]), r=[("YB", i)], w=[f"ybt{s}"], dma=True)

    def loadx2(i):
        s = i % 2
        L = cur["L"]; xin = cur["xin"]
        xkey = [] if L == 0 else [("X", L % 2, i)]
        P.op("sp", lambda: nc.sync.dma_start(out=xt[s][:], in_=xin[i * 128:(i + 1) * 128, :]), r=xkey, w=[f"xt{s}"], dma=True)

    def yassemble(i):
        s = i % 2
        P.op("dve", lambda: nc.vector.tensor_tensor(out=hs[:], in0=hdir[s][:], in1=h1t[s][:], op=ALU.add), r=[f"hdir{s}", f"h1t{s}"], w=["hs"])
        P.op("dve", lambda: nc.vector.scalar_tensor_tensor(out=hs[:], in0=so_t[s][:], scalar=1.0, in1=hs[:], op0=ALU.add, op1=ALU.mult), r=["hs", f"so{s}"], w=["hs"])
        for h in range(NH):
            P.op("act", (lambda h=h: nc.scalar.activation(out=hsq[:], in_=hs[:, h * 128:(h + 1) * 128], func=AF.Square, accum_out=hss[:, h:h + 1])), r=["hs"], w=["hsq", ("hss", h)])
        HSS = [("hss", h) for h in range(NH)]
        P.op("act", lambda: nc.scalar.activation(out=hl[:], in_=hss[:], func=AF.Ln, scale=1.0 / 128, bias=eps4b[:]), r=HSS + ["eps4b"], w=["hl"])
        P.op("act", lambda: nc.scalar.activation(out=hr[:], in_=hl[:], func=AF.Exp, scale=-0.5), r=["hl"], w=["hr"])

    def yassemble2(i):
        s = i % 2
        P.op("dve", lambda: nc.vector.scalar_tensor_tensor(out=mz[:], in0=sz_t[s][:], scalar=0.5, in1=mhnw[:], op0=ALU.mult, op1=ALU.mult), r=[f"sz{s}", "mhnw"], w=["mz"])
        P.op("dve", lambda: nc.vector.tensor_tensor(out=hs[:].rearrange("p (h d) -> p h d", h=4), in0=hs[:].rearrange("p (h d) -> p h d", h=4),
                                                    in1=hr[:].unsqueeze(2).broadcast_to([128, 4, 128]), op=ALU.mult), r=["hs", "hr"], w=["hs"])
        P.op("dve", lambda: nc.vector.tensor_tensor(out=ycat[:, 0:512], in0=hs[:], in1=mz[:], op=ALU.mult), r=["hs", "mz"], w=[("hn", 0)])
        P.op("act", lambda: nc.scalar.copy(out=ycat[:, 512:1024], in_=ybt[s][:]), r=[f"ybt{s}"], w=[("hn", 1)])

    def finish_a(i):
        s = i % 2
        for cc in range(8):
            P.op("pe", (lambda cc=cc: nc.tensor.transpose(out=tpb[:, cc * 128:(cc + 1) * 128], in_=ycat[:, cc * 128:(cc + 1) * 128], identity=identb[:])),
                 r=HN + ["identb"], w=["b0"])
        P.op("act", lambda: nc.scalar.copy(out=yT[:], in_=tpb.rearrange("p (k t) -> p k t", k=8)), r=["b0"], w=["hnT"])
        outproj_half(i, 0)

    def outproj_half(i, half):
        s = i % 2
        for cc in range(8):
            P.op("pe", (lambda cc=cc: nc.tensor.matmul(pj[half][:, :], lhsT=yT[:, cc, :], rhs=wout[:, cc, half * 512:(half + 1) * 512], start=(cc == 0), stop=(cc == 7))),
                 r=["hnT", ("wout", cc)], w=[f"pj{half}"])
        P.op("dve", lambda: nc.vector.tensor_tensor(out=xo[s][:, half * 512:(half + 1) * 512], in0=pj[half][:, :], in1=xt[s][:, half * 512:(half + 1) * 512], op=ALU.add),
             r=[f"pj{half}", f"xt{s}"], w=[f"xt{s}"])

    def finish_b(i):
        s = i % 2
        L = cur["L"]; final = cur["final"]; xout = cur["xout"]
        outproj_half(i, 1)
        if final:
            P.op("act", lambda: nc.scalar.activation(out=apT[:, 0, 0:2, :].rearrange("p a c -> p (a c)"), in_=xo[s][:], func=AF.Square, accum_out=ss[:]), r=[f"xt{s}"], w=[("apT", 0, 0), ("apT", 0, 1), "ss"])
            P.op("act", lambda: nc.scalar.activation(out=lnv[:], in_=ss[:], func=AF.Ln, scale=1.0 / D, bias=epsb[:]), r=["ss", "epsb"], w=["lnv"])
            P.op("act", lambda: nc.scalar.activation(out=rstd[:], in_=lnv[:], func=AF.Exp, scale=-0.5), r=["lnv"], w=["rstd"])
            P.op("dve", lambda: nc.vector.scalar_tensor_tensor(out=xo[s][:], in0=xo[s][:], scalar=rstd[:], in1=fnw[:], op0=ALU.mult, op1=ALU.mult), r=[f"xt{s}", "rstd", "btmp2"], w=[f"xt{s}"])
            P.op("sp", lambda: nc.sync.dma_start(out=xout[i * 128:(i + 1) * 128, :], in_=xo[s][:]), r=[f"xt{s}"], dma=True)
        else:
            P.op("sp", lambda: nc.sync.dma_start(out=xout[i * 128:(i + 1) * 128, :], in_=xo[s][:]), r=[f"xt{s}"], w=[("X", (L + 1) % 2, i)], dma=True)
            if i == NT - 1:
                P.op("sp", lambda: nc.sync.dma_start(out=cch_in[:, :], in_=xo[s][:]), r=[f"xt{s}"], w=["cch_in"], dma=True)
                P.op("pool", lambda: nc.gpsimd.collective_compute("AllGather", ALU.bypass, replica_groups=RG, ins=[cch_in[:, :]], outs=[cch_out[:, :]]),
                     r=["cch_in"], w=["cch_out"], dma="cc")

    def pass2_tile(i):
        s = i % 2
        if i > 0:
            load2(i - 1)
        loadx2(i)
        mlstm_H1(2, s, vx2[s], f"vx2{s}")
        mlstm_H1b()
        if i + 1 <= NT - 1:
            finish_a(i + 1)
        mlstm_H2(vx2[s], f"vx2{s}", hdir[s], f"hdir{s}")
        mlstm_cast()
        if i > 0:
            mlstm_G(i - 1, 2)
        yassemble(i)
        if i + 1 <= NT - 1:
            finish_b(i + 1)
        if i > 0:
            mlstm_Gb()
            mlstm_K((i - 1) % 2)
        yassemble2(i)

    for L in range(depth):
        P.epoch = L
        cur["L"] = L
        cur["final"] = (L == depth - 1)
        cur["xin"] = x_d if L == 0 else Xs[L % 2]
        cur["xout"] = xo_d if L == depth - 1 else Xs[(L + 1) % 2]
        if L > 0:
            load_params_p1(L)
        P.op("dve", lambda: nc.vector.memset(stt[:], 0.0), w=["Cst"])
        P.op("dve", lambda: nc.vector.memset(stb16[:], 0.0), w=["Cb"])
        P.op("dve", lambda: nc.vector.memset(mst[0][:], 0.0), w=["mst0"])
        load_x1(0)
        load_x1(1)
        stageA1n(0)
        stageA1p()
        stageA1f(0)
        stageA2(0)
        stageA1n(1, act_head=lambda: mlstm_G_head(0))
        stageA1p()
        for i in range(NT):
            pass1_tile(i)
        mfin = mst[NT % 2]; mfk = f"mst{NT % 2}"
        P.op("sp", lambda: nc.sync.dma_start(out=ccs_in[0:128, :], in_=stt[:]), r=["Cst"], w=["ccs_in"], dma=True)
        P.op("sp", (lambda mfin=mfin: nc.sync.dma_start(out=ccs_in[128:129, 0:4].rearrange("o c -> c o"), in_=mfin[:])), r=[mfk], w=["ccs_in_m"], dma=True)
        P.op("pool", lambda: nc.gpsimd.collective_compute("AllGather", ALU.bypass, replica_groups=RG, ins=[ccs_in[:, :]], outs=[ccs_out[:, :]]),
             r=["ccs_in", "ccs_in_m"], w=["ccs_out"], dma="cc")
        if L + 1 < depth:
            load_win(L + 1)
        if L > 0:
            load_params_p2(L)
        if cur["final"]:
            P.op("sp", lambda: nc.sync.dma_start(out=fnw[:], in_=fnw_d[:, :]), w=["btmp2"], dma=True)
        m2 = mst[(NT - 1) % 2]; m2k = f"mst{(NT - 1) % 2}"
        P.op("sp", lambda: nc.sync.dma_start(out=stt[:], in_=ccs_out[0:128, :]), r=["ccs_out"], w=["Cst"], dma=True)
        P.op("sp", lambda: nc.sync.dma_start(out=st2[:], in_=ccs_out[129:257, :]), r=["ccs_out"], w=["st2"], dma=True)
        P.op("sp", (lambda m2=m2: nc.sync.dma_start(out=m2[:], in_=ccs_out[128:129, 0:4].rearrange("o c -> c o"))), r=["ccs_out"], w=[m2k], dma=True)
        P.op("sp", lambda: nc.sync.dma_start(out=mtmp[:], in_=ccs_out[257:258, 0:4].rearrange("o c -> c o")), r=["ccs_out"], w=["mtmp"], dma=True)
        P.op("dve", lambda: nc.vector.tensor_scalar(out=st2[:], in0=st2[:], scalar1=msel[:, 1:2], scalar2=None, op0=ALU.mult), r=["st2", "msel"], w=["st2"])
        P.op("dve", lambda: nc.vector.scalar_tensor_tensor(out=stt[:], in0=stt[:], scalar=msel[:, 0:1], in1=st2[:], op0=ALU.mult, op1=ALU.add), r=["Cst", "st2", "msel"], w=["Cst"])
        P.op("act", lambda: nc.scalar.copy(out=stb16[:], in_=stt[:]), r=["Cst"], w=["Cb"])
        P.op("dve", lambda: nc.vector.tensor_scalar(out=mtmp[:], in0=mtmp[:], scalar1=msel[0:4, 1:2], scalar2=None, op0=ALU.mult), r=["mtmp", "msel"], w=["mtmp"])
        P.op("dve", (lambda m2=m2: nc.vector.scalar_tensor_tensor(out=m2[:], in0=m2[:], scalar=msel[0:4, 0:1], in1=mtmp[:], op0=ALU.mult, op1=ALU.add)), r=[m2k, "mtmp", "msel"], w=[m2k])
        softplus_all_dir2()
        load2(NT - 1)
        mlstm_G(NT - 1, 2)
        mlstm_Gb()
        mlstm_K((NT - 1) % 2)
        for i in range(NT - 1, -1, -1):
            pass2_tile(i)
        finish_a(0)
        finish_b(0)
        if L + 1 < depth:
            load_wout(L + 1)
    if stop is not None:
        P.ops = P.ops[:stop]
        P.op_epoch = P.op_epoch[:stop]
    stats = P.emit()
    return nc, stats, dbg_outs


def t5_bucket_np(rel):
    nb = 16
    max_exact = 8
    ret = np.where(rel > 0, nb, 0)
    n = np.abs(rel)
    nf = np.maximum(n, 1).astype(np.float32)
    large = max_exact + (np.log(nf / max_exact) / np.log(128 / max_exact) * (nb - max_exact)).astype(np.int32)
    large = np.minimum(large, nb - 1)
    return ret + np.where(n < max_exact, n, large)


def make_consts():
    c = np.zeros((128, 1024), np.float32)
    s = np.arange(128)[:, None]; t = np.arange(128)[None, :]
    c[:, 0:128] = np.eye(128)
    c[:, 128:256] = (s <= t)
    c[:, 256:384] = (s >= t)
    c[:, 384:512] = np.where(s <= t, 0.0, MASKNEG)
    c[:, 512:640] = np.where(s >= t, 0.0, MASKNEG)
    selc = np.zeros((4, 644), np.float32)
    for h in range(4):
        selc[h, h * 128:(h + 1) * 128] = 1.0
    selc[:, 512:516] = np.eye(4)
    selc[:, 516:644] = 1.0
    return c, selc


def make_bias_tables(rel_bias, flipped):
    k = np.arange(128)[:, None]; q = np.arange(128)[None, :]
    tabs = []
    for jj in range(3):
        rel = (jj - 1) * 128 + k - q
        ok = np.abs(rel) <= 128
        bk = t5_bucket_np(-rel if flipped else rel)
        vals = rel_bias[bk]
        vals = np.where(ok[:, :, None], vals, np.float32(MASKNEG)).astype(np.float32)
        tabs.append(np.ascontiguousarray(vals.transpose(0, 2, 1)).reshape(128, 1024))
    tabs.append(np.ascontiguousarray(tabs[2][::-1]))
    return np.stack(tabs).astype(np.float32)


def col_perm(flipped):
    A = 512
    qk = np.arange(0, 1024); v_a = np.arange(1024, 1536); o_a = np.arange(1536, 2048); z_a = np.arange(2048, 2560)
    g = np.arange(2560, 2576); q_b = np.arange(2576, 3088); k_b = np.arange(3088, 3216); v_b = np.arange(3216, 3344)
    z_b = np.arange(3344, 3856)
    qb_perm = np.concatenate([np.concatenate([q_b[c * 64:(c + 1) * 64], q_b[(4 + c) * 64:(5 + c) * 64]]) for c in range(4)])
    i_f, i_b, f_f, f_b = g[0:4], g[4:8], g[8:12], g[12:16]
    gord = np.concatenate([i_b, f_b, i_f, f_f]) if flipped else np.concatenate([i_f, f_f, i_b, f_b])
    perm = np.concatenate([qk, qb_perm, k_b, v_a, o_a, z_a, z_b, v_b, gord])
    assert perm.shape[0] == NCOL
    return perm, (gord - 2560)


def layer_inputs(inputs, L, flipped):
    perm, gord = col_perm(flipped)
    cw = inputs["conv_w"][L]
    if flipped:
        cw = cw[::-1]
    d = {
        "w_in": np.ascontiguousarray(inputs["w_in"][L][:, perm]),
        "w_out": np.ascontiguousarray(inputs["w_out"][L]),
        "normw": np.ascontiguousarray(inputs["norm_w"][L].reshape(8, 128).T),
        "convw": np.ascontiguousarray(cw.reshape(5, 8, 128).transpose(2, 1, 0).reshape(128, 40)),
        "convb": np.ascontiguousarray(inputs["conv_b"][L].reshape(8, 128).T),
        "gateb": np.ascontiguousarray(np.broadcast_to(inputs["gate_b"][L][gord][None, :], (128, 16))),
        "mhnw": np.ascontiguousarray(np.broadcast_to(inputs["mhn_w"][L][None, :], (128, 512))),
        "sink": np.ascontiguousarray(np.broadcast_to(inputs["sink"][L][None, :], (128, 8))),
        "fnw": np.ascontiguousarray(np.broadcast_to(inputs["final_norm_w"][None, :], (128, D))),
    }
    return d


_CACHE = {}


DBG = False
LAST = {}


def get_prog(NT, depth):
    key = (NT, depth, DBG)
    if key not in _CACHE:
        _CACHE[key] = build(NT, depth, dbg=DBG)[0]
    return _CACHE[key]


def kernel(x, norm_w, w_in, conv_w, conv_b, gate_b, mhn_w, sink, rel_bias, w_out, final_norm_w):
    inputs = dict(x=np.asarray(x), norm_w=np.asarray(norm_w), w_in=np.asarray(w_in), conv_w=np.asarray(conv_w),
                  conv_b=np.asarray(conv_b), gate_b=np.asarray(gate_b), mhn_w=np.asarray(mhn_w), sink=np.asarray(sink),
                  rel_bias=np.asarray(rel_bias), w_out=np.asarray(w_out), final_norm_w=np.asarray(final_norm_w))
    B, S, _ = inputs["x"].shape
    half = S // 2
    NT = half // 128
    depth = inputs["w_in"].shape[0]
    cst, selc = make_consts()
    bias = [make_bias_tables(inputs["rel_bias"], f) for f in (False, True)]
    lay = []
    for f in (False, True):
        per = [layer_inputs(inputs, L, f) for L in range(depth)]
        d = {k: np.ascontiguousarray(np.stack([p[k] for p in per])) for k in per[0] if k != "fnw"}
        d["fnw"] = per[0]["fnw"]
        lay.append(d)
    xs = []
    for c in range(8):
        b, hf = c // 2, c % 2
        seg = inputs["x"][b, hf * half:(hf + 1) * half]
        xs.append(np.ascontiguousarray(seg[::-1] if hf else seg))
    maps = []
    for c in range(8):
        m = dict(lay[c % 2])
        msel = np.zeros((128, 2), np.float32)
        msel[:, 1 - (c % 2)] = 1.0
        m.update({"x": xs[c], "xh": np.ascontiguousarray(xs[c ^ 1][(NT - 1) * 128:]), "biasT": bias[c % 2],
                  "consts": cst, "selc": selc, "msel": msel})
        maps.append(m)
    nc = get_prog(NT, depth)
    res = run_bass_kernel_spmd(nc, maps, core_ids=list(range(8)))
    LAST["res"] = res
    out = np.empty((B, S, D), np.float32)
    for c in range(8):
        b, hf = c // 2, c % 2
        o = np.asarray(res.results[c]["x_out"])
        out[b, hf * half:(hf + 1) * half] = o[::-1] if hf else o
    return out
```

```python
import numpy as np
import ml_dtypes
import concourse.bass as bass
import concourse.mybir as mybir
from concourse.bass_utils import run_bass_kernel_spmd

F32 = mybir.dt.float32
BF16 = mybir.dt.bfloat16
AF = mybir.ActivationFunctionType
ALU = mybir.AluOpType

D = 1024
DEPTH = 4
NH = 4
EPS = 1e-6
LN_SCALE = float(np.log(128.0 ** -0.5))
MASKNEG = -30000.0
NCOL = 3856
C_QA, C_KA, C_QB, C_KB = 0, 512, 1024, 1536
C_VA, C_OA, C_ZA, C_ZB, C_VB, C_G = 1664, 2176, 2688, 3200, 3712, 3840


class Prog:
    ENGS = ("pe", "act", "dve", "pool", "sp")

    def __init__(self, nc, n_dma_sems=14, same_engine_sync=True):
        self.nc = nc
        self.ops = []
        self.epoch = 0
        self.op_epoch = []
        self.n_dma_sems = n_dma_sems
        self.same_engine_sync = same_engine_sync

    PSUM_KEYS = frozenset(["b0", "pj0", "pj1", "cvb", "B4", "B5", "B6", "B7"])

    def op(self, eng, fn, r=(), w=(), dma=False):
        r = tuple(r); w = tuple(w)
        pr = tuple(k for k in r if k in self.PSUM_KEYS)
        if pr:
            r = tuple(k for k in r if k not in self.PSUM_KEYS)
            w = w + pr
        self.ops.append((eng, fn, r, w, dma))
        self.op_epoch.append(self.epoch)

    def emit(self):
        nc = self.nc
        engs = {"pe": nc.tensor, "act": nc.scalar, "dve": nc.vector, "pool": nc.gpsimd, "sp": nc.sync}
        ops = self.ops
        n = len(ops)
        lastw, readers = {}, {}
        deps = [None] * n
        for i, (eng, fn, r, w, dma) in enumerate(ops):
            d = set()
            for k in r:
                j = lastw.get(k)
                if j is not None:
                    d.add(j)
            for k in w:
                j = lastw.get(k)
                if j is not None:
                    d.add(j)
                for j in readers.get(k, ()):
                    d.add(j)
            for k in r:
                readers.setdefault(k, []).append(i)
            for k in w:
                lastw[k] = i
                readers[k] = []
            d.discard(i)
            deps[i] = d
        dma_sem_of, pools, last_on_sem, dma_val, sem_handles = {}, {}, {}, {}, {}
        for i, (eng, fn, r, w, dma) in enumerate(ops):
            if not dma:
                continue
            pk = (eng, "cc") if dma == "cc" else eng
            p = pools.setdefault(pk, [0])
            s = (pk, p[0] % (2 if dma == "cc" else self.n_dma_sems))
            p[0] += 1
            inc = 1 if dma == "cc" else 16
            if s not in sem_handles:
                sem_handles[s] = nc.alloc_semaphore(name=f"d_{eng}_{'cc' if dma == 'cc' else 'q'}_{s[1]}")
            prev = last_on_sem.get(s)
            if prev is not None:
                deps[i].add(prev)
                dma_val[i] = dma_val[prev] + inc
            else:
                dma_val[i] = inc
            last_on_sem[s] = i
            dma_sem_of[i] = s
        marked = [False] * n
        for i in range(n):
            ei = ops[i][0]
            for j in deps[i]:
                if ops[j][4]:
                    continue
                ej = ops[j][0]
                if ej != ei or (self.same_engine_sync and ej != "pe"):
                    marked[j] = True
        ep = self.op_epoch
        esem = {}
        rank = [0] * n
        cnt = {}
        for i in range(n):
            k = (ops[i][0], ep[i])
            if k not in esem:
                esem[k] = nc.alloc_semaphore(name=f"c_{k[0]}_{k[1]}")
                cnt[k] = 0
            if marked[i] and not ops[i][4]:
                cnt[k] += 1
            rank[i] = cnt[k]
        seen = {e: {} for e in self.ENGS}
        nw = 0
        for i, (eng, fn, r, w, dma) in enumerate(ops):
            E = engs[eng]
            need = {}
            for j in deps[i]:
                if ops[j][4]:
                    key = ("d",) + dma_sem_of[j]
                    sem, val = sem_handles[dma_sem_of[j]], dma_val[j]
                else:
                    ej = ops[j][0]
                    if ej == eng and not (self.same_engine_sync and ej != "pe"):
                        continue
                    key = ("c", ej, ep[j])
                    sem, val = esem[(ej, ep[j])], rank[j]
                if need.get(key, (None, 0))[1] < val:
                    need[key] = (sem, val)
            for key, (sem, val) in need.items():
                if seen[eng].get(key, 0) < val:
                    E.wait_ge(sem, val)
                    seen[eng][key] = val
                    nw += 1
            inst = fn()
            if dma == "cc":
                inst.then_inc(sem_handles[dma_sem_of[i]], 1)
            elif dma:
                inst.then_inc(sem_handles[dma_sem_of[i]], 16)
            elif marked[i]:
                inst.then_inc(esem[(eng, ep[i])], 1)
        for s, j in last_on_sem.items():
            key = ("d",) + s
            if seen["sp"].get(key, 0) < dma_val[j]:
                nc.sync.wait_ge(sem_handles[s], dma_val[j])
        return {"ops": n, "waits": nw, "marks": sum(marked), "max_sem": max(cnt.values())}


def build(NT, depth=DEPTH, dbg=False, stop=None):
    nc = bass.Bass("TRN2", target_bir_lowering=False)
    P = Prog(nc)
    T = NT * 128

    def din(name, shape, dt=F32):
        return nc.dram_tensor(name, shape, dt, kind="ExternalInput").ap()

    def dout(name, shape, dt=F32):
        return nc.dram_tensor(name, shape, dt, kind="ExternalOutput").ap()

    def dscr(name, shape, dt=F32):
        return nc.dram_tensor(name, shape, dt, kind="Internal").ap()

    def sb(name, shape, dt=F32):
        return nc.alloc_sbuf_tensor("s_" + name, shape, dt)

    x_d = din("x", [T, D])
    xh_d = din("xh", [128, D])
    win_d = din("w_in", [depth, D, NCOL])
    wout_d = din("w_out", [depth, D, D])
    normw_d = din("normw", [depth, 128, 8])
    convw_d = din("convw", [depth, 128, 40])
    convb_d = din("convb", [depth, 128, 8])
    gateb_d = din("gateb", [depth, 128, 16])
    mhnw_d = din("mhnw", [depth, 128, 512])
    sink_d = din("sink", [depth, 128, 8])
    fnw_d = din("fnw", [128, D])
    msel_d = din("msel", [128, 2])
    bias_d = din("biasT", [4, 128, 1024])
    cst_d = din("consts", [128, 1024])
    xo_d = dout("x_out", [T, D])
    Xs = [dscr("Xs0", [T, D]), dscr("Xs1", [T, D])]
    ccs_in = dscr("ccs_in", [129, 516]); ccs_out = dscr("ccs_out", [258, 516])
    cch_in = dscr("cch_in", [128, D]); cch_out = dscr("cch_out", [256, D])
    RG = [[0, 1], [2, 3], [4, 5], [6, 7]]
    cur = {}
    H1_d = dscr("H1", [NT, 128, 512])
    SO_d = dscr("SO", [NT, 128, 512])
    SZ_d = dscr("SZ", [NT, 128, 512])
    YB_d = dscr("YB", [NT, 128, 512], BF16)
    QT_d = dscr("QT", [NT, 128, 512], BF16)
    KT_d = dscr("KT", [NT, 128, 512], BF16)
    VX_d = dscr("VX", [NT, 128, 512], BF16)
    dbg_outs = {}

    cst = sb("cst", [128, 1024])
    ident = cst[:, 0:128]
    U = {1: cst[:, 128:256], 2: cst[:, 256:384]}
    MN = {1: cst[:, 384:512], 2: cst[:, 512:640]}
    sel = cst[0:4, 640:1152] if False else None
    P.op("sp", lambda: nc.sync.dma_start(out=cst[:], in_=cst_d[:, :]), w=["cst"], dma=True)
    selc = sb("selc", [4, 4 * 128 + 4 + 128])
    seld = din("selc", [4, 4 * 128 + 4 + 128])
    P.op("sp", lambda: nc.sync.dma_start(out=selc[:], in_=seld[:, :]), w=["selc"], dma=True)
    I4 = selc[:, 512:516]
    ones4 = selc[:, 516:644]
    identb = sb("identb", [128, 128], BF16)
    P.op("dve", lambda: nc.vector.tensor_copy(out=identb[:], in_=ident), r=["cst"], w=["identb"])

    win = sb("win", [128, 8, NCOL], BF16)
    wout = sb("wout", [128, 8, D], BF16)
    def load_win(L):
        win_v = win_d[L].rearrange("(kc p) c -> p kc c", p=128)
        for kc in range(8):
            P.op("pool", (lambda kc=kc: nc.gpsimd.dma_start(out=win[:, kc, :], in_=win_v[:, kc, :])), w=[("win", kc)], dma=True)

    def load_wout(L):
        wout_v = wout_d[L].rearrange("(kc p) c -> p kc c", p=128)
        for kc in range(8):
            P.op("pool", (lambda kc=kc: nc.gpsimd.dma_start(out=wout[:, kc, :], in_=wout_v[:, kc, :])), w=[("wout", kc)], dma=True)
    load_win(0)
    load_wout(0)
    msel = sb("msel", [128, 2])
    P.op("sp", lambda: nc.sync.dma_start(out=msel[:], in_=msel_d[:, :]), w=["msel"], dma=True)
    WIN_R = [("win", kc) for kc in range(8)]
    WOUT_R = [("wout", kc) for kc in range(8)]

    normw = sb("normw", [128, 8]); convw = sb("convw", [128, 40]); convb = sb("convb", [128, 8])
    gateb = sb("gateb", [128, 16]); mhnw = sb("mhnw", [128, 512]); sinkt = sb("sinkt", [128, 8])
    esink = sb("esink", [128, 8])
    cdiag = sb("cdiag", [128, 40, 128], BF16)

    def load_params_p1(L):
        for t, d_, nm in [(normw, normw_d, "normw"), (convw, convw_d, "convw"), (convb, convb_d, "convb"),
                          (gateb, gateb_d, "gateb"), (sinkt, sink_d, "sinkt")]:
            P.op("sp", (lambda t=t, d_=d_: nc.sync.dma_start(out=t[:], in_=d_[L, :, :])), w=[nm], dma=True)
        P.op("act", lambda: nc.scalar.activation(out=esink[:], in_=sinkt[:], func=AF.Exp), r=["sinkt"], w=["esink"])
        for j in range(40):
            P.op("dve", (lambda j=j: nc.vector.tensor_scalar(out=cdiag[:, j, :], in0=ident, scalar1=convw[:, j:j + 1], scalar2=None, op0=ALU.mult)),
                 r=["cst", "convw"], w=[("cdiag", j)])

    def load_params_p2(L):
        P.op("sp", lambda: nc.sync.dma_start(out=mhnw[:], in_=mhnw_d[L, :, :]), w=["mhnw"], dma=True)
    load_params_p1(0)
    load_params_p2(0)
    ebias = sb("ebias", [128, 4, 1024])
    btmp = sb("btmp", [128, 1024]); btmp2 = sb("btmp2", [128, 1024])
    for tb in range(4):
        P.op("sp", (lambda tb=tb: nc.sync.dma_start(out=btmp[:], in_=bias_d[tb, :, :])), w=["aex0", "aex1"], dma=True)
        P.op("act", (lambda tb=tb: nc.scalar.activation(out=ebias[:, tb, :], in_=btmp[:], func=AF.Exp)), r=["aex0", "aex1"], w=[("ebias", tb)])

    banks = [nc.alloc_psum_tensor(f"bank{i}", [128, 512], F32) for i in range(8)]
    tpb = banks[0][:, :].bitcast(BF16)
    pj = [banks[1], banks[2]]
    ktp = banks[1][:, 0:256].bitcast(BF16)
    cvb = banks[3]
    gps = banks[4]
    Xs4 = banks[4][:, 256:268]
    tps = banks[4][:, 268:284]; dec_ps = banks[4][:, 284:288]
    cmb = banks[5]
    stb = banks[6]
    itb = banks[7]
    o_ps = banks[7][:, 0:260]

    NS = 2
    xt = [sb(f"xt{s}", [128, D]) for s in range(2)]
    junk = hn_junk = None
    ss = sb("ss", [128, 1]); lnv = sb("lnv", [128, 1]); rstd = sb("rstd", [128, 1])
    hn = sb("hn", [128, D], BF16)
    HN = [("hn", 0), ("hn", 1)]
    hnT = sb("hnT", [128, 8, 128], BF16)
    raw = [sb(f"raw{s}", [128, 8, 132], BF16) for s in range(NS)]
    qbT = [sb(f"qbT{s}", [128, 4, 128], BF16) for s in range(NS)]
    kbT = [[sb(f"kbT{s}_{g}", [128, 128], BF16) for g in range(2)] for s in range(4)]
    vbx = [sb(f"vbx{s}", [128, 2, 65], BF16) for s in range(4)]
    vext = [sb(f"vext{s}", [128, 4, 128], BF16) for s in range(NS)]
    szb = [sb(f"szb{s}", [128, 512]) for s in range(NS)]
    so_t = [sb(f"so{s}", [128, 512]) for s in range(2)]
    sz_t = [sb(f"sz{s}", [128, 512]) for s in range(2)]
    gall = sb("gall", [128, NT + 1, 16])
    qT = [sb(f"qT{s}", [128, 4, 128], BF16) for s in range(2)]
    kT = [sb(f"kT{s}", [128, 4, 128], BF16) for s in range(2)]
    onesb = sb("onesb", [128, 1], BF16)
    P.op("dve", lambda: nc.vector.memset(onesb[:], 1.0), w=["onesb"])
    for s in range(NS):
        P.op("dve", (lambda s=s: nc.vector.memset(raw[s][:], 0.0)), w=[f"raw{s}"])
    for s in range(4):
        P.op("dve", (lambda s=s: nc.vector.memset(vbx[s][:], 1.0)), w=[f"vbx{s}"])
        for g in range(2):
            P.op("dve", (lambda s=s, g=g: nc.vector.memset(kbT[s][g][:], 0.0)), w=[f"kbT{s}"])
    stt = sb("stt", [128, 516]); stb16 = sb("stb16", [128, 516], BF16)
    Cst = stt[:, 0:512].rearrange("p (h d) -> p h d", h=4); nst = stt[:, 512:516]
    Cb = stb16[:, 0:512].rearrange("p (h d) -> p h d", h=4); nb = stb16[:, 512:516]
    st2 = sb("st2", [128, 516])
    mst = [sb(f"mst{s}", [NH, 1]) for s in range(2)]
    ge = sb("ge", [128, 4]); gsp = sb("gsp", [128, 4])
    gsb = sb("gsb", [NH, 2, 128]); gg = sb("gg", [NH, 128]); Rg = sb("Rg", [NH, 4, 128])
    NG = sb("NG", [NH, 128]); dl = sb("dl", [NH, 1]); dd = sb("dd", [NH, 4]); glast = sb("glast", [NH, 1])
    tok = sb("tok", [128, 16]); decay = sb("decay", [128, 4])
    arg = [sb(f"arg{h}", [128, 128]) for h in range(NH)]
    pT = [sb(f"pT{h}", [128, 128], BF16) for h in range(NH)]
    itmp = sb("itmp", [128, 512]); iden = sb("iden", [128, 4])
    hu = sb("hu", [128, 512]); dn = sb("dn", [128, 4]); rdn = sb("rdn", [128, 4])
    hdir = [sb(f"hdir{s}", [128, 512]) for s in range(2)]
    kw = sb("kw", [128, NH, 128], BF16)
    apT = sb("apT", [128, 2, 3, 512], BF16)
    aex = [btmp[:, 0:512], btmp[:, 512:1024]]
    aden = sb("aden", [128, 4]); arden = sb("arden", [128, 4]); atmp = sb("atmp", [128, 4, 64])
    yb = [sb(f"yb{s}", [128, 512], BF16) for s in range(2)]
    tht = sb("tht", [128, 512])

    def dbg_out(name, ap_sb, shape, keys, dt=F32):
        if not dbg:
            return
        o = dout("dbg_" + name, shape, dt)
        dbg_outs[name] = o
        P.op("sp", lambda: nc.sync.dma_start(out=o, in_=ap_sb), r=keys, dma=True)

    def load_x1(i):
        xs = i % 2
        L = cur["L"]
        if i == NT and L > 0:
            P.op("sp", lambda: nc.sync.dma_start(out=xt[xs][:], in_=cch_out[0:128, :]), r=["cch_out"], w=[f"xt{xs}"], dma=True)
            P.op("sp", lambda: nc.sync.dma_start(out=btmp[:], in_=cch_out[128:256, :]), r=["cch_out"], w=["aex0", "aex1"], dma=True)
            P.op("dve", lambda: nc.vector.tensor_scalar(out=btmp[:], in0=btmp[:], scalar1=msel[:, 1:2], scalar2=None, op0=ALU.mult), r=["aex0", "aex1", "msel"], w=["aex0", "aex1"])
            P.op("dve", lambda: nc.vector.scalar_tensor_tensor(out=xt[xs][:], in0=xt[xs][:], scalar=msel[:, 0:1], in1=btmp[:], op0=ALU.mult, op1=ALU.add), r=[f"xt{xs}", "msel", "aex0", "aex1"], w=[f"xt{xs}"])
        else:
            src = xh_d[:, :] if i == NT else cur["xin"][i * 128:(i + 1) * 128, :]
            xkey = [] if (i == NT or L == 0) else [("X", L % 2, i)]
            P.op("sp", lambda: nc.sync.dma_start(out=xt[xs][:], in_=src), r=xkey, w=[f"xt{xs}"], dma=True)

    def stageA1n(i, act_head=None):
        xs = i % 2
        P.op("act", lambda: nc.scalar.activation(out=hn[:], in_=xt[xs][:], func=AF.Square, accum_out=ss[:]), r=[f"xt{xs}"], w=HN + ["ss"])
        if act_head is not None:
            act_head()
        P.op("act", lambda: nc.scalar.activation(out=lnv[:], in_=ss[:], func=AF.Ln, scale=1.0 / D, bias=epsb[:]), r=["ss", "epsb"], w=["lnv"])
        P.op("act", lambda: nc.scalar.activation(out=rstd[:], in_=lnv[:], func=AF.Exp, scale=-0.5), r=["lnv"], w=["rstd"])
        P.op("dve", lambda: nc.vector.tensor_scalar(out=hn[:], in0=xt[xs][:], scalar1=rstd[:], scalar2=None, op0=ALU.mult), r=[f"xt{xs}", "rstd"], w=HN)

    def stageA1p():
        for kc in range(8):
            P.op("pe", (lambda kc=kc: nc.tensor.transpose(out=tpb[:, kc * 128:(kc + 1) * 128], in_=hn[:, kc * 128:(kc + 1) * 128], identity=identb[:])),
                 r=HN + ["identb"], w=["b0"])
        P.op("dve", lambda: nc.vector.tensor_tensor(out=hnT[:], in0=tpb.rearrange("p (k t) -> p k t", k=8),
                                                    in1=normw[:].unsqueeze(2).broadcast_to([128, 8, 128]), op=ALU.mult),
             r=["b0", "normw"], w=["hnT"])

    def stageA1f(i):
        xs = i % 2; s3 = i % NS; s4 = i % 4
        groups = [(0, 4), (4, 8), (8, 12), (12, 13)]
        for gi, (c0, c1) in enumerate(groups):
            bk = pj[gi % 2]; bkey = f"pj{gi % 2}"
            for c in range(c0, c1):
                for kc in range(8):
                    P.op("pe", (lambda c=c, kc=kc, bk=bk, c0=c0: nc.tensor.matmul(bk[:, (c - c0) * 128:(c - c0 + 1) * 128], lhsT=win[:, kc, c * 128:(c + 1) * 128],
                                                                              rhs=hnT[:, kc, :], start=(kc == 0), stop=(kc == 7))),
                         r=["hnT", ("win", kc)], w=[bkey])
            if gi < 2:
                P.op("act", (lambda bk=bk, c0=c0: nc.scalar.copy(out=raw[s3][:, c0:c0 + 4, 2:130], in_=bk[:, :].rearrange("p (c t) -> p c t", c=4))),
                     r=[bkey], w=[f"raw{s3}"])
            elif gi == 2:
                P.op("act", (lambda bk=bk: nc.scalar.activation(out=qbT[s3][:], in_=bk[:, :].rearrange("p (c t) -> p c t", c=4), func=AF.Copy, scale=0.125)),
                     r=[bkey], w=[f"qbT{s3}"])
            else:
                for g in range(2):
                    P.op("act", (lambda bk=bk, g=g: nc.scalar.copy(out=kbT[s4][g][64 * g:64 * g + 64, :], in_=bk[64 * g:64 * g + 64, 0:128])), r=[bkey], w=[f"kbT{s4}"])
    def stageA2(i):
        xs = i % 2; s3 = i % NS; s4 = i % 4
        tg = [(C_VA, 512, "va"), (C_OA, 512, "oa"), (C_ZA, 512, "za"), (C_ZB, 512, "zb"), (C_VB, 144, "vbg")]
        for gi, (cs, wd, nm) in enumerate(tg):
            if i == NT and nm in ("oa", "za", "zb", "va"):
                continue
            bk = pj[gi % 2]; bkey = f"pj{gi % 2}"
            for kc in range(8):
                P.op("pe", (lambda kc=kc, bk=bk, cs=cs, wd=wd: nc.tensor.matmul(bk[:, 0:wd], lhsT=hnT[:, kc, :], rhs=win[:, kc, cs:cs + wd], start=(kc == 0), stop=(kc == 7))),
                     r=["hnT", ("win", kc)], w=[bkey])
            if nm == "va":
                P.op("dve", (lambda bk=bk: nc.vector.tensor_copy(out=vext[s3][:], in_=bk[:, :].rearrange("p (h d) -> p h d", h=4))), r=[bkey], w=[f"vext{s3}"])
            elif nm == "oa":
                P.op("act", (lambda bk=bk: nc.scalar.activation(out=so_t[xs][:], in_=bk[:, :], func=AF.Tanh, scale=0.5)), r=[bkey], w=[f"so{xs}"])
                P.op("sp", lambda: nc.sync.dma_start(out=SO_d[i, :, :], in_=so_t[xs][:]), r=[f"so{xs}"], w=[("SO", i)], dma=True)
            elif nm == "za":
                P.op("act", (lambda bk=bk: nc.scalar.activation(out=tht[:], in_=bk[:, :], func=AF.Tanh, scale=0.5)), r=[bkey], w=["tht"])
                P.op("dve", (lambda bk=bk: nc.vector.scalar_tensor_tensor(out=sz_t[xs][:], in0=tht[:], scalar=1.0, in1=bk[:, :], op0=ALU.add, op1=ALU.mult)), r=["tht", bkey], w=[f"sz{xs}"])
                P.op("sp", lambda: nc.sync.dma_start(out=SZ_d[i, :, :], in_=sz_t[xs][:]), r=[f"sz{xs}"], w=[("SZ", i)], dma=True)
            elif nm == "zb":
                P.op("act", (lambda bk=bk: nc.scalar.activation(out=tht[:], in_=bk[:, :], func=AF.Tanh, scale=0.5)), r=[bkey], w=["tht"])
                P.op("dve", (lambda bk=bk: nc.vector.scalar_tensor_tensor(out=szb[s3][:], in0=tht[:], scalar=1.0, in1=bk[:, :], op0=ALU.add, op1=ALU.mult)), r=["tht", bkey], w=[f"szb{s3}"])
            else:
                P.op("dve", (lambda bk=bk: nc.vector.tensor_copy(out=vbx[s4][:, :, 0:64], in_=bk[:, 0:128].rearrange("p (g d) -> p g d", g=2))), r=[bkey], w=[f"vbx{s4}"])
                P.op("dve", (lambda bk=bk: nc.vector.tensor_tensor(out=gall[:, i, :], in0=bk[:, 128:144], in1=gateb[:], op=ALU.add)), r=[bkey, "gateb"], w=[("gall", i)])

    epsb = sb("epsb", [128, 1])
    P.op("dve", lambda: nc.vector.memset(epsb[:], EPS), w=["epsb"])
    eps4b = sb("eps4b", [128, 1])
    P.op("dve", lambda: nc.vector.memset(eps4b[:], 4.0 * EPS), w=["eps4b"])
    lnsb = sb("lnsb", [NH, 1])
    P.op("dve", lambda: nc.vector.memset(lnsb[:], LN_SCALE), w=["lnsb"])

    def conv(i):
        s3 = i % NS; sn = (i + 1) % NS; sq = i % 2
        if i == 0:
            P.op("dve", lambda: nc.vector.memset(raw[s3][:, :, 0:2], 0.0), w=[f"raw{s3}"])
        if i == NT - 1:
            P.op("dve", lambda: nc.vector.tensor_copy(out=raw[s3][:, :, 130:131], in_=raw[sn][:, :, 129:130]), r=[f"raw{sn}"], w=[f"raw{s3}"])
            P.op("dve", lambda: nc.vector.tensor_copy(out=raw[s3][:, :, 131:132], in_=raw[sn][:, :, 128:129]), r=[f"raw{sn}"], w=[f"raw{s3}"])
        else:
            P.op("dve", lambda: nc.vector.tensor_copy(out=raw[s3][:, :, 130:132], in_=raw[sn][:, :, 2:4]), r=[f"raw{sn}"], w=[f"raw{s3}"])
            P.op("dve", lambda: nc.vector.tensor_copy(out=raw[sn][:, :, 0:2], in_=raw[s3][:, :, 128:130]), r=[f"raw{s3}"], w=[f"raw{sn}"])
        for half in range(2):
            cbank, ckey = (cvb, "cvb") if half == 0 else (stb, "B6")
            for c4 in range(4):
                cc = half * 4 + c4
                for k in range(5):
                    P.op("pe", (lambda cc=cc, c4=c4, k=k, cbank=cbank: nc.tensor.matmul(cbank[:, c4 * 128:(c4 + 1) * 128], lhsT=cdiag[:, cc * 5 + k, :], rhs=raw[s3][:, cc, k:k + 128],
                                                                                     start=(k == 0), stop=(k == 4))),
                         r=[f"raw{s3}", ("cdiag", cc * 5 + k)], w=[ckey])
        for half in range(2):
            cbank, ckey = (cvb, "cvb") if half == 0 else (stb, "B6")
            dst = qT[sq] if half == 0 else kT[sq]
            dkey = f"qT{sq}" if half == 0 else f"kT{sq}"
            for c4 in range(4):
                cc = half * 4 + c4
                P.op("act", (lambda cc=cc, c4=c4, dst=dst, cbank=cbank: nc.scalar.activation(out=dst[:, c4, :], in_=cbank[:, c4 * 128:(c4 + 1) * 128], func=AF.Silu, bias=convb[:, cc:cc + 1])),
                     r=[ckey, "convb"], w=[dkey])

    gsp2 = sb("gsp2", [128, NT, 4])

    def mlstm_G_head(i):
        P.op("act", lambda: nc.scalar.activation(out=ge[:], in_=gall[:, i, 4:8], func=AF.Exp, scale=-1.0), r=[("gall", i)], w=["ge"])
        P.op("act", lambda: nc.scalar.activation(out=gsp[:], in_=ge[:], func=AF.Ln, bias=1.0), r=["ge"], w=["gsp"])

    def softplus_all_dir2():
        GALL = [("gall", i) for i in range(NT)]
        P.op("act", lambda: nc.scalar.activation(out=gsp2[:], in_=gall[:, 0:NT, 12:16], func=AF.Exp, scale=-1.0), r=GALL, w=["gsp2"])
        P.op("act", lambda: nc.scalar.activation(out=gsp2[:], in_=gsp2[:], func=AF.Ln, bias=1.0), r=["gsp2"], w=["gsp2"])

    def mlstm_G(i, d, head_done=False):
        fwd = (d == 1)
        gc = 8 * (d - 1)
        last = 127 if fwd else 0
        mp = mst[i % 2]; mn = mst[(i + 1) % 2]
        mpk = f"mst{i % 2}"; mnk = f"mst{(i + 1) % 2}"
        if d == 1:
            spv = gsp[:]; spk = "gsp"
            if not head_done:
                mlstm_G_head(i)
        else:
            spv = gsp2[:, i, :]; spk = "gsp2"
        P.op("pe", lambda: nc.tensor.matmul(gps[0:4, 0:128], lhsT=gall[:, i, gc:gc + 4], rhs=ident, start=True, stop=False), r=[("gall", i), "cst"], w=["B4"])
        P.op("pe", lambda: nc.tensor.matmul(gps[0:4, 0:128], lhsT=spv, rhs=U[d], start=False, stop=True), r=[spk, "cst"], w=["B4"])
        P.op("pe", lambda: nc.tensor.matmul(gps[0:4, 128:256], lhsT=spv, rhs=U[d], start=True, stop=True), r=[spk, "cst"], w=["B4"])
        P.op("dve", lambda: nc.vector.tensor_copy(out=gsb[:], in_=gps[0:4, 0:256].rearrange("p (a t) -> p a t", a=2)), r=["B4"], w=["gsb"])
        a_ap = gsb[:, 0, :]; nb_ap = gsb[:, 1, :]
        if fwd:
            P.op("dve", lambda: nc.vector.tensor_tensor_scan(out=gg[:], data0=ones4, data1=a_ap, initial=mp[:], op0=ALU.mult, op1=ALU.max), r=["gsb", "selc", mpk], w=["gg"])
        else:
            P.op("dve", lambda: nc.vector.tensor_tensor_scan(out=gg[:, ::-1], data0=ones4, data1=gsb[:, 0, ::-1], initial=mp[:], op0=ALU.mult, op1=ALU.max), r=["gsb", "selc", mpk], w=["gg"])
        P.op("dve", lambda: nc.vector.tensor_copy(out=glast[:], in_=gg[:, last:last + 1]), r=["gg"], w=["glast"])
        P.op("dve", lambda: nc.vector.tensor_tensor(out=mn[:], in0=gg[:, last:last + 1], in1=gsb[:, 1, last:last + 1], op=ALU.subtract), r=["gg", "gsb"], w=[mnk])
        P.op("dve", lambda: nc.vector.tensor_scalar(out=Rg[:, 0, :], in0=a_ap, scalar1=lnsb[:], scalar2=None, op0=ALU.add), r=["gsb", "lnsb"], w=["Rg0"])
        P.op("dve", lambda: nc.vector.tensor_scalar(out=Rg[:, 1, :], in0=gg[:], scalar1=-1.0, scalar2=mp[:], op0=ALU.mult, op1=ALU.add), r=["gg", mpk], w=["Rg1"])
        P.op("dve", lambda: nc.vector.tensor_tensor(out=Rg[:, 2, :], in0=nb_ap, in1=gg[:], op=ALU.subtract), r=["gsb", "gg"], w=["Rg2"])
        P.op("dve", lambda: nc.vector.tensor_scalar(out=Rg[:, 3, :], in0=a_ap, scalar1=glast[:], scalar2=lnsb[:], op0=ALU.subtract, op1=ALU.add), r=["gsb", "glast", "lnsb"], w=["Rg3"])
        P.op("dve", lambda: nc.vector.tensor_scalar(out=NG[:], in0=gg[:], scalar1=-1.0, scalar2=None, op0=ALU.mult), r=["gg"], w=["NG"])
        P.op("dve", lambda: nc.vector.tensor_copy(out=dl[:], in_=Rg[:, 1, last:last + 1]), r=["Rg1"], w=["dl"])
        P.op("dve", lambda: nc.vector.tensor_scalar(out=dd[:], in0=I4, scalar1=dl[:], scalar2=None, op0=ALU.mult), r=["dl", "selc"], w=["dd"])
        P.op("act", lambda: nc.scalar.activation(out=Rg[:, 1:4, :], in_=Rg[:, 1:4, :], func=AF.Exp), r=["Rg1", "Rg2", "Rg3"], w=["Rg1", "Rg2", "Rg3"])

    def mlstm_Gb():
        for h in range(NH):
            P.op("pe", (lambda h=h: nc.tensor.matmul(cmb[:, h * 128:(h + 1) * 128], lhsT=selc[:, h * 128:(h + 1) * 128], rhs=NG[:], start=True, stop=True)), r=["NG", "selc"], w=["B5"])
        P.op("pe", lambda: nc.tensor.matmul(dec_ps, lhsT=ones4, rhs=dd[:], start=True, stop=True), r=["dd", "selc"], w=["B4"])
        for q in range(4):
            P.op("pe", (lambda q=q: nc.tensor.transpose(out=tps[:, q * 4:(q + 1) * 4], in_=Rg[:, q, :], identity=I4)), r=[f"Rg{q}", "selc"], w=["B4"])
        P.op("dve", lambda: nc.vector.tensor_copy(out=tok[:], in_=tps), r=["B4"], w=["tok"])
        P.op("act", lambda: nc.scalar.activation(out=decay[:], in_=dec_ps, func=AF.Exp), r=["B4"], w=["decay"])

    def mlstm_K(sq):
        for h in range(NH):
            P.op("pe", (lambda h=h: nc.tensor.transpose(out=ktp[:, h * 128:(h + 1) * 128], in_=kT[sq][:, h, :], identity=identb[:])), r=[f"kT{sq}", "identb"], w=["pj0"])
        P.op("dve", lambda: nc.vector.tensor_tensor(out=kw[:], in0=ktp.rearrange("p (h d) -> p h d", h=4), in1=tok[:, 12:16].unsqueeze(2).broadcast_to([128, 4, 128]), op=ALU.mult),
             r=["pj0", "tok"], w=["kw"])

    def mlstm_H1(d, sq, sv, vkey):
        for h in range(NH):
            P.op("pe", (lambda h=h: nc.tensor.matmul(stb[:, h * 128:(h + 1) * 128], lhsT=kT[sq][:, h, :], rhs=qT[sq][:, h, :], start=True, stop=True)), r=[f"kT{sq}", f"qT{sq}"], w=["B6"])
        for h in range(NH):
            P.op("pe", (lambda h=h: nc.tensor.matmul(itb[:, h * 128:(h + 1) * 128], lhsT=qT[sq][:, h, :], rhs=Cb[:, h, :], start=True, stop=True)), r=[f"qT{sq}", "Cb"], w=["B7"])
            P.op("pe", (lambda h=h: nc.tensor.matmul(Xs4[:, h:h + 1], lhsT=qT[sq][:, h, :], rhs=nb[:, h:h + 1], start=True, stop=True)), r=[f"qT{sq}", "Cb"], w=["B4"])
        for h in range(NH):
            P.op("pe", (lambda h=h: nc.tensor.matmul(cvb[:, h * 128:(h + 1) * 128], lhsT=kw[:, h, :], rhs=sv[:, h, :], start=True, stop=True)), r=["kw", vkey], w=["cvb"])
            P.op("pe", (lambda h=h: nc.tensor.matmul(Xs4[:, 4 + h:5 + h], lhsT=kw[:, h, :], rhs=onesb[:], start=True, stop=True)), r=["kw", "onesb"], w=["B4"])
        for h in range(NH):
            P.op("dve", (lambda h=h: nc.vector.scalar_tensor_tensor(out=arg[h][:], in0=cmb[:, h * 128:(h + 1) * 128], scalar=tok[:, h:h + 1], in1=MN[d], op0=ALU.add, op1=ALU.add)),
                 r=["B5", "tok", "cst"], w=[f"arg{h}"])
            P.op("act", (lambda h=h: nc.scalar.activation(out=arg[h][:], in_=arg[h][:], func=AF.Exp)), r=[f"arg{h}"], w=[f"arg{h}"])
        for h in range(NH):
            P.op("dve", (lambda h=h: nc.vector.tensor_tensor(out=pT[h][:], in0=arg[h][:], in1=stb[:, h * 128:(h + 1) * 128], op=ALU.mult)), r=[f"arg{h}", "B6"], w=[f"pT{h}"])
        P.op("dve", lambda: nc.vector.tensor_tensor(out=itmp[:].rearrange("p (h d) -> p h d", h=4), in0=itb[:, :].rearrange("p (h d) -> p h d", h=4),
                                                    in1=tok[:, 4:8].unsqueeze(2).broadcast_to([128, 4, 128]), op=ALU.mult), r=["B7", "tok"], w=["itmp"])
        P.op("dve", lambda: nc.vector.tensor_tensor(out=iden[:], in0=Xs4[:, 0:4], in1=tok[:, 4:8], op=ALU.mult), r=["B4", "tok"], w=["iden"])

    def mlstm_H1b():
        for h in range(NH):
            P.op("dve", (lambda h=h: nc.vector.scalar_tensor_tensor(out=Cst[:, h, :], in0=Cst[:, h, :], scalar=decay[:, h:h + 1], in1=cvb[:, h * 128:(h + 1) * 128], op0=ALU.mult, op1=ALU.add)),
                 r=["Cst", "decay", "cvb"], w=["Cst"])
        P.op("dve", lambda: nc.vector.tensor_tensor(out=nst, in0=nst, in1=decay[:], op=ALU.mult), r=["Cst", "decay"], w=["Cst"])
        P.op("dve", lambda: nc.vector.tensor_tensor(out=nst, in0=nst, in1=Xs4[:, 4:8], op=ALU.add), r=["Cst", "B4"], w=["Cst"])

    def mlstm_H2(sv, vkey, hout, hkey):
        for h in range(NH):
            P.op("pe", (lambda h=h: nc.tensor.matmul(stb[:, h * 128:(h + 1) * 128], lhsT=pT[h][:], rhs=sv[:, h, :], start=True, stop=True)), r=[f"pT{h}", vkey], w=["B6"])
            P.op("pe", (lambda h=h: nc.tensor.matmul(Xs4[:, 8 + h:9 + h], lhsT=pT[h][:], rhs=onesb[:], start=True, stop=True)), r=[f"pT{h}", "onesb"], w=["B4"])
        P.op("dve", lambda: nc.vector.tensor_tensor(out=hu[:], in0=itmp[:], in1=stb[:, :], op=ALU.add), r=["itmp", "B6"], w=["hu"])
        P.op("dve", lambda: nc.vector.tensor_tensor(out=iden[:], in0=iden[:], in1=Xs4[:, 8:12], op=ALU.add), r=["iden", "B4"], w=["iden"])
        P.op("dve", lambda: nc.vector.tensor_tensor(out=dn[:], in0=iden[:], in1=tok[:, 8:12], op=ALU.max), r=["iden", "tok"], w=["dn"])
        P.op("dve", lambda: nc.vector.scalar_tensor_tensor(out=dn[:], in0=iden[:], scalar=-1.0, in1=dn[:], op0=ALU.mult, op1=ALU.max), r=["iden", "dn"], w=["dn"])
        P.op("dve", lambda: nc.vector.reciprocal(out=rdn[:], in_=dn[:]), r=["dn"], w=["rdn"])
        P.op("dve", lambda: nc.vector.tensor_tensor(out=hout[:, :].rearrange("p (h d) -> p h d", h=4), in0=hu[:].rearrange("p (h d) -> p h d", h=4),
                                                    in1=rdn[:].unsqueeze(2).broadcast_to([128, 4, 128]), op=ALU.mult), r=["hu", "rdn"], w=[hkey])

    def mlstm_cast():
        P.op("act", lambda: nc.scalar.copy(out=stb16[:], in_=stt[:]), r=["Cst"], w=["Cb"])

    def att_blocks(i):
        blocks = []
        if i > 0:
            blocks.append(((i - 1) % 4, 0))
        blocks.append((i % 4, 1))
        blocks.append(((i + 1) % 4, 3 if i == NT - 1 else 2))
        return blocks

    sbanks = [(pj[0], "pj0"), (pj[1], "pj1"), (banks[0], "b0"), (banks[5], "B5")]

    def attention_S(i):
        s3 = i % NS
        blocks = att_blocks(i)
        n = 0
        for g in range(2):
            for bi, (ks, tb) in enumerate(blocks):
                bk, bkey = sbanks[n % 4]; ax = n % 2
                n += 1
                P.op("pe", (lambda bk=bk, ks=ks, g=g: nc.tensor.matmul(bk[:, :], lhsT=kbT[ks][g][:], rhs=qbT[s3][:], start=True, stop=True)), r=[f"kbT{ks}", f"qbT{s3}"], w=[bkey])
                P.op("act", (lambda bk=bk, ax=ax: nc.scalar.activation(out=aex[ax], in_=bk[:, :], func=AF.Exp)), r=[bkey], w=[f"aex{ax}"])
                P.op("pool", (lambda bi=bi, tb=tb, g=g, ax=ax: nc.gpsimd.tensor_tensor(out=apT[:, g, bi, :], in0=aex[ax], in1=ebias[:, tb, g * 512:(g + 1) * 512], op=ALU.mult)),
                     r=[f"aex{ax}", ("ebias", tb)], w=[("apT", g, bi)])

    def attention_O(i):
        s3 = i % NS; ys = i % 2
        blocks = att_blocks(i)
        nb = len(blocks)

        def group_pe(g):
            ob = (o_ps, "B7") if g == 0 else (cvb[:, 0:260], "cvb")
            for c in range(4):
                for bi, (ks, tb) in enumerate(blocks):
                    P.op("pe", (lambda c=c, bi=bi, ks=ks: nc.tensor.matmul(ob[0][:, c * 65:(c + 1) * 65], lhsT=apT[:, g, bi, c * 128:(c + 1) * 128], rhs=vbx[ks][:, g, :],
                                                                        start=(bi == 0), stop=(bi == nb - 1))),
                         r=[("apT", g, bi), f"vbx{ks}"], w=[ob[1]])

        def group(g):
            ob = (o_ps, "B7") if g == 0 else (cvb[:, 0:260], "cvb")
            o3 = ob[0].rearrange("p (c e) -> p c e", c=4)
            P.op("dve", lambda: nc.vector.tensor_tensor(out=aden[:], in0=o3[:, :, 64], in1=esink[:, 4 * g:4 * g + 4], op=ALU.add), r=[ob[1], "esink"], w=["aden"])
            P.op("dve", lambda: nc.vector.reciprocal(out=arden[:], in_=aden[:]), r=["aden"], w=["arden"])
            P.op("dve", lambda: nc.vector.tensor_tensor(out=atmp[:], in0=o3[:, :, 0:64], in1=arden[:].unsqueeze(2).broadcast_to([128, 4, 64]), op=ALU.mult), r=[ob[1], "arden"], w=["atmp"])
            P.op("dve", lambda: nc.vector.scalar_tensor_tensor(out=yb[ys][:, g * 256:(g + 1) * 256], in0=atmp[:].rearrange("p c e -> p (c e)"), scalar=0.5, in1=szb[s3][:, g * 256:(g + 1) * 256], op0=ALU.mult, op1=ALU.mult),
                 r=["atmp", f"szb{s3}"], w=[f"yb{ys}"])
        group_pe(0)
        group_pe(1)
        group(0)
        group(1)
        P.op("sp", lambda: nc.sync.dma_start(out=YB_d[i, :, :], in_=yb[ys][:]), r=[f"yb{ys}"], w=[("YB", i)], dma=True)

    def pass1_tile(i):
        sq = i % 2; s3 = i % NS
        if i + 2 <= NT:
            load_x1(i + 2)
        mlstm_G(i, 1, head_done=True)
        stageA1f(i + 1)
        mlstm_Gb()
        conv(i)
        P.op("sp", lambda: nc.sync.dma_start(out=QT_d[i, :, :], in_=qT[sq][:].rearrange("p h t -> p (h t)")), r=[f"qT{sq}"], w=[("QT", i)], dma=True)
        P.op("sp", lambda: nc.sync.dma_start(out=KT_d[i, :, :], in_=kT[sq][:].rearrange("p h t -> p (h t)")), r=[f"kT{sq}"], w=[("KT", i)], dma=True)
        P.op("sp", lambda: nc.sync.dma_start(out=VX_d[i, :, :], in_=vext[s3][:].rearrange("p h t -> p (h t)")), r=[f"vext{s3}"], w=[("VX", i)], dma=True)
        stageA2(i + 1)
        mlstm_K(sq)
        mlstm_H1(1, sq, vext[s3], f"vext{s3}")
        if i + 2 <= NT:
            stageA1n(i + 2, act_head=lambda: mlstm_G_head(i + 1))
        elif i + 1 <= NT - 1:
            mlstm_G_head(i + 1)
        mlstm_H1b()
        attention_S(i)
        mlstm_H2(vext[s3], f"vext{s3}", hdir[sq], f"hdir{sq}")
        if i + 2 <= NT:
            stageA1p()
        P.op("sp", lambda: nc.sync.dma_start(out=H1_d[i, :, :], in_=hdir[sq][:]), r=[f"hdir{sq}"], w=[("H1", i)], dma=True)
        attention_O(i)
        mlstm_cast()

    h1t = [sb(f"h1t{s}", [128, 512]) for s in range(2)]
    ybt = [sb(f"ybt{s}", [128, 512], BF16) for s in range(2)]
    vx2 = [sb(f"vx2{s}", [128, 4, 128], BF16) for s in range(2)]
    hs = sb("hs", [128, 512]); hsq = sb("hsq", [128, 128]); hss = sb("hss", [128, 4]); hl = sb("hl", [128, 4]); hr = sb("hr", [128, 4])
    mz = sb("mz", [128, 512])
    ycat = hn; yT = hnT
    xo = xt
    fnw = btmp2
    mtmp = sb("mtmp", [NH, 1])

    def load2(i):
        s = i % 2
        P.op("sp", lambda: nc.sync.dma_start(out=qT[s][:].rearrange("p h t -> p (h t)"), in_=QT_d[i, :, :]), r=[("QT", i)], w=[f"qT{s}"], dma=True)
        P.op("sp", lambda: nc.sync.dma_start(out=kT[s][:].rearrange("p h t -> p (h t)"), in_=KT_d[i, :, :]), r=[("KT", i)], w=[f"kT{s}"], dma=True)
        P.op("sp", lambda: nc.sync.dma_start(out=vx2[s][:].rearrange("p h t -> p (h t)"), in_=VX_d[i, :, :]), r=[("VX", i)], w=[f"vx2{s}"], dma=True)
        P.op("sp", lambda: nc.sync.dma_start(out=h1t[s][:], in_=H1_d[i, :, :]), r=[("H1", i)], w=[f"h1t{s}"], dma=True)
        P.op("sp", lambda: nc.sync.dma_start(out=so_t[s][:], in_=SO_d[i, :, :]), r=[("SO", i)], w=[f"so{s}"], dma=True)
        P.op("sp", lambda: nc.sync.dma_start(out=sz_t[s][:], in_=SZ_d[i, :, :]), r=[("SZ", i)], w=[f"sz{s}"], dma=True)
        P.op("sp", lambda: nc.sync.dma_start(out=ybt[s][:], in_=YB_d[i, :, :]), r=[("YB", i)], w=[f"ybt{s}"], dma=True)

    def loadx2(i):
        s = i % 2
        L = cur["L"]; xin = cur["xin"]
        xkey = [] if L == 0 else [("X", L % 2, i)]
        P.op("sp", lambda: nc.sync.dma_start(out=xt[s][:], in_=xin[i * 128:(i + 1) * 128, :]), r=xkey, w=[f"xt{s}"], dma=True)

    def yassemble(i):
        s = i % 2
        P.op("dve", lambda: nc.vector.tensor_tensor(out=hs[:], in0=hdir[s][:], in1=h1t[s][:], op=ALU.add), r=[f"hdir{s}", f"h1t{s}"], w=["hs"])
        P.op("dve", lambda: nc.vector.scalar_tensor_tensor(out=hs[:], in0=so_t[s][:], scalar=1.0, in1=hs[:], op0=ALU.add, op1=ALU.mult), r=["hs", f"so{s}"], w=["hs"])
        for h in range(NH):
            P.op("act", (lambda h=h: nc.scalar.activation(out=hsq[:], in_=hs[:, h * 128:(h + 1) * 128], func=AF.Square, accum_out=hss[:, h:h + 1])), r=["hs"], w=["hsq", ("hss", h)])
        HSS = [("hss", h) for h in range(NH)]
        P.op("act", lambda: nc.scalar.activation(out=hl[:], in_=hss[:], func=AF.Ln, scale=1.0 / 128, bias=eps4b[:]), r=HSS + ["eps4b"], w=["hl"])
        P.op("act", lambda: nc.scalar.activation(out=hr[:], in_=hl[:], func=AF.Exp, scale=-0.5), r=["hl"], w=["hr"])

    def yassemble2(i):
        s = i % 2
        P.op("dve", lambda: nc.vector.scalar_tensor_tensor(out=mz[:], in0=sz_t[s][:], scalar=0.5, in1=mhnw[:], op0=ALU.mult, op1=ALU.mult), r=[f"sz{s}", "mhnw"], w=["mz"])
        P.op("dve", lambda: nc.vector.tensor_tensor(out=hs[:].rearrange("p (h d) -> p h d", h=4), in0=hs[:].rearrange("p (h d) -> p h d", h=4),
                                                    in1=hr[:].unsqueeze(2).broadcast_to([128, 4, 128]), op=ALU.mult), r=["hs", "hr"], w=["hs"])
        P.op("dve", lambda: nc.vector.tensor_tensor(out=ycat[:, 0:512], in0=hs[:], in1=mz[:], op=ALU.mult), r=["hs", "mz"], w=[("hn", 0)])
        P.op("act", lambda: nc.scalar.copy(out=ycat[:, 512:1024], in_=ybt[s][:]), r=[f"ybt{s}"], w=[("hn", 1)])

    def finish_a(i):
        s = i % 2
        for cc in range(8):
            P.op("pe", (lambda cc=cc: nc.tensor.transpose(out=tpb[:, cc * 128:(cc + 1) * 128], in_=ycat[:, cc * 128:(cc + 1) * 128], identity=identb[:])),
                 r=HN + ["identb"], w=["b0"])
        P.op("act", lambda: nc.scalar.copy(out=yT[:], in_=tpb.rearrange("p (k t) -> p k t", k=8)), r=["b0"], w=["hnT"])
        outproj_half(i, 0)

    def outproj_half(i, half):
        s = i % 2
        for cc in range(8):
            P.op("pe", (lambda cc=cc: nc.tensor.matmul(pj[half][:, :], lhsT=yT[:, cc, :], rhs=wout[:, cc, half * 512:(half + 1) * 512], start=(cc == 0), stop=(cc == 7))),
                 r=["hnT", ("wout", cc)], w=[f"pj{half}"])
        P.op("dve", lambda: nc.vector.tensor_tensor(out=xo[s][:, half * 512:(half + 1) * 512], in0=pj[half][:, :], in1=xt[s][:, half * 512:(half + 1) * 512], op=ALU.add),
             r=[f"pj{half}", f"xt{s}"], w=[f"xt{s}"])

    def finish_b(i):
        s = i % 2
        L = cur["L"]; final = cur["final"]; xout = cur["xout"]
        outproj_half(i, 1)
        if final:
            P.op("act", lambda: nc.scalar.activation(out=apT[:, 0, 0:2, :].rearrange("p a c -> p (a c)"), in_=xo[s][:], func=AF.Square, accum_out=ss[:]), r=[f"xt{s}"], w=[("apT", 0, 0), ("apT", 0, 1), "ss"])
            P.op("act", lambda: nc.scalar.activation(out=lnv[:], in_=ss[:], func=AF.Ln, scale=1.0 / D, bias=epsb[:]), r=["ss", "epsb"], w=["lnv"])
            P.op("act", lambda: nc.scalar.activation(out=rstd[:], in_=lnv[:], func=AF.Exp, scale=-0.5), r=["lnv"], w=["rstd"])
            P.op("dve", lambda: nc.vector.scalar_tensor_tensor(out=xo[s][:], in0=xo[s][:], scalar=rstd[:], in1=fnw[:], op0=ALU.mult, op1=ALU.mult), r=[f"xt{s}", "rstd", "btmp2"], w=[f"xt{s}"])
            P.op("sp", lambda: nc.sync.dma_start(out=xout[i * 128:(i + 1) * 128, :], in_=xo[s][:]), r=[f"xt{s}"], dma=True)
        else:
            P.op("sp", lambda: nc.sync.dma_start(out=xout[i * 128:(i + 1) * 128, :], in_=xo[s][:]), r=[f"xt{s}"], w=[("X", (L + 1) % 2, i)], dma=True)
            if i == NT - 1:
                P.op("sp", lambda: nc.sync.dma_start(out=cch_in[:, :], in_=xo[s][:]), r=[f"xt{s}"], w=["cch_in"], dma=True)
                P.op("pool", lambda: nc.gpsimd.collective_compute("AllGather", ALU.bypass, replica_groups=RG, ins=[cch_in[:, :]], outs=[cch_out[:, :]]),
                     r=["cch_in"], w=["cch_out"], dma="cc")

    def pass2_tile(i):
        s = i % 2
        if i > 0:
            load2(i - 1)
        loadx2(i)
        mlstm_H1(2, s, vx2[s], f"vx2{s}")
        mlstm_H1b()
        if i + 1 <= NT - 1:
            finish_a(i + 1)
        mlstm_H2(vx2[s], f"vx2{s}", hdir[s], f"hdir{s}")
        mlstm_cast()
        if i > 0:
            mlstm_G(i - 1, 2)
        yassemble(i)
        if i + 1 <= NT - 1:
            finish_b(i + 1)
        if i > 0:
            mlstm_Gb()
            mlstm_K((i - 1) % 2)
        yassemble2(i)

    P.op("dve", lambda: nc.vector.memset(st2[:], 0.0), w=["st2"])
    P.op("sp", lambda: nc.sync.dma_start(out=ccs_in[128:129, :], in_=st2[0:1, :]), r=["st2"], w=["ccs_in_m"], dma=True)

    for L in range(depth):
        P.epoch = L
        cur["L"] = L
        cur["final"] = (L == depth - 1)
        cur["xin"] = x_d if L == 0 else Xs[L % 2]
        cur["xout"] = xo_d if L == depth - 1 else Xs[(L + 1) % 2]
        if L > 0:
            load_params_p1(L)
        P.op("dve", lambda: nc.vector.memset(stt[:], 0.0), w=["Cst"])
        P.op("dve", lambda: nc.vector.memset(stb16[:], 0.0), w=["Cb"])
        P.op("dve", lambda: nc.vector.memset(mst[0][:], 0.0), w=["mst0"])
        load_x1(0)
        load_x1(1)
        stageA1n(0)
        stageA1p()
        stageA1f(0)
        stageA2(0)
        stageA1n(1, act_head=lambda: mlstm_G_head(0))
        stageA1p()
        for i in range(NT):
            pass1_tile(i)
        mfin = mst[NT % 2]; mfk = f"mst{NT % 2}"
        P.op("sp", lambda: nc.sync.dma_start(out=ccs_in[0:128, :], in_=stt[:]), r=["Cst"], w=["ccs_in"], dma=True)
        P.op("sp", (lambda mfin=mfin: nc.sync.dma_start(out=ccs_in[128:129, 0:4].rearrange("o c -> c o"), in_=mfin[:])), r=[mfk], w=["ccs_in_m"], dma=True)
        P.op("pool", lambda: nc.gpsimd.collective_compute("AllGather", ALU.bypass, replica_groups=RG, ins=[ccs_in[:, :]], outs=[ccs_out[:, :]]),
             r=["ccs_in", "ccs_in_m"], w=["ccs_out"], dma="cc")
        if L + 1 < depth:
            load_win(L + 1)
        if L > 0:
            load_params_p2(L)
        if cur["final"]:
            P.op("sp", lambda: nc.sync.dma_start(out=fnw[:], in_=fnw_d[:, :]), w=["btmp2"], dma=True)
        m2 = mst[(NT - 1) % 2]; m2k = f"mst{(NT - 1) % 2}"
        P.op("sp", lambda: nc.sync.dma_start(out=stt[:], in_=ccs_out[0:128, :]), r=["ccs_out"], w=["Cst"], dma=True)
        P.op("sp", lambda: nc.sync.dma_start(out=st2[:], in_=ccs_out[129:257, :]), r=["ccs_out"], w=["st2"], dma=True)
        P.op("sp", (lambda m2=m2: nc.sync.dma_start(out=m2[:], in_=ccs_out[128:129, 0:4].rearrange("o c -> c o"))), r=["ccs_out"], w=[m2k], dma=True)
        P.op("sp", lambda: nc.sync.dma_start(out=mtmp[:], in_=ccs_out[257:258, 0:4].rearrange("o c -> c o")), r=["ccs_out"], w=["mtmp"], dma=True)
        P.op("dve", lambda: nc.vector.tensor_scalar(out=st2[:], in0=st2[:], scalar1=msel[:, 1:2], scalar2=None, op0=ALU.mult), r=["st2", "msel"], w=["st2"])
        P.op("dve", lambda: nc.vector.scalar_tensor_tensor(out=stt[:], in0=stt[:], scalar=msel[:, 0:1], in1=st2[:], op0=ALU.mult, op1=ALU.add), r=["Cst", "st2", "msel"], w=["Cst"])
        P.op("act", lambda: nc.scalar.copy(out=stb16[:], in_=stt[:]), r=["Cst"], w=["Cb"])
        P.op("dve", lambda: nc.vector.tensor_scalar(out=mtmp[:], in0=mtmp[:], scalar1=msel[0:4, 1:2], scalar2=None, op0=ALU.mult), r=["mtmp", "msel"], w=["mtmp"])
        P.op("dve", (lambda m2=m2: nc.vector.scalar_tensor_tensor(out=m2[:], in0=m2[:], scalar=msel[0:4, 0:1], in1=mtmp[:], op0=ALU.mult, op1=ALU.add)), r=[m2k, "mtmp", "msel"], w=[m2k])
        softplus_all_dir2()
        load2(NT - 1)
        mlstm_G(NT - 1, 2)
        mlstm_Gb()
        mlstm_K((NT - 1) % 2)
        for i in range(NT - 1, -1, -1):
            pass2_tile(i)
        finish_a(0)
        finish_b(0)
        if L + 1 < depth:
            load_wout(L + 1)
    if stop is not None:
        P.ops = P.ops[:stop]
        P.op_epoch = P.op_epoch[:stop]
    stats = P.emit()
    return nc, stats, dbg_outs


def t5_bucket_np(rel):
    nb = 16
    max_exact = 8
    ret = np.where(rel > 0, nb, 0)
    n = np.abs(rel)
    nf = np.maximum(n, 1).astype(np.float32)
    large = max_exact + (np.log(nf / max_exact) / np.log(128 / max_exact) * (nb - max_exact)).astype(np.int32)
    large = np.minimum(large, nb - 1)
    return ret + np.where(n < max_exact, n, large)


def make_consts():
    c = np.zeros((128, 1024), np.float32)
    s = np.arange(128)[:, None]; t = np.arange(128)[None, :]
    c[:, 0:128] = np.eye(128)
    c[:, 128:256] = (s <= t)
    c[:, 256:384] = (s >= t)
    c[:, 384:512] = np.where(s <= t, 0.0, MASKNEG)
    c[:, 512:640] = np.where(s >= t, 0.0, MASKNEG)
    selc = np.zeros((4, 644), np.float32)
    for h in range(4):
        selc[h, h * 128:(h + 1) * 128] = 1.0
    selc[:, 512:516] = np.eye(4)
    selc[:, 516:644] = 1.0
    return c, selc


def make_bias_tables(rel_bias, flipped):
    k = np.arange(128)[:, None]; q = np.arange(128)[None, :]
    tabs = []
    for jj in range(3):
        rel = (jj - 1) * 128 + k - q
        ok = np.abs(rel) <= 128
        bk = t5_bucket_np(-rel if flipped else rel)
        vals = rel_bias[bk]
        vals = np.where(ok[:, :, None], vals, np.float32(MASKNEG)).astype(np.float32)
        tabs.append(np.ascontiguousarray(vals.transpose(0, 2, 1)).reshape(128, 1024))
    tabs.append(np.ascontiguousarray(tabs[2][::-1]))
    return np.stack(tabs).astype(np.float32)


def col_perm(flipped):
    A = 512
    qk = np.arange(0, 1024); v_a = np.arange(1024, 1536); o_a = np.arange(1536, 2048); z_a = np.arange(2048, 2560)
    g = np.arange(2560, 2576); q_b = np.arange(2576, 3088); k_b = np.arange(3088, 3216); v_b = np.arange(3216, 3344)
    z_b = np.arange(3344, 3856)
    qb_perm = np.concatenate([np.concatenate([q_b[c * 64:(c + 1) * 64], q_b[(4 + c) * 64:(5 + c) * 64]]) for c in range(4)])
    i_f, i_b, f_f, f_b = g[0:4], g[4:8], g[8:12], g[12:16]
    gord = np.concatenate([i_b, f_b, i_f, f_f]) if flipped else np.concatenate([i_f, f_f, i_b, f_b])
    perm = np.concatenate([qk, qb_perm, k_b, v_a, o_a, z_a, z_b, v_b, gord])
    assert perm.shape[0] == NCOL
    return perm, (gord - 2560)


def layer_inputs(inputs, L, flipped):
    perm, gord = col_perm(flipped)
    cw = inputs["conv_w"][L]
    if flipped:
        cw = cw[::-1]
    d = {
        "w_in": np.ascontiguousarray(inputs["w_in"][L][:, perm]),
        "w_out": np.ascontiguousarray(inputs["w_out"][L]),
        "normw": np.ascontiguousarray(inputs["norm_w"][L].reshape(8, 128).T),
        "convw": np.ascontiguousarray(cw.reshape(5, 8, 128).transpose(2, 1, 0).reshape(128, 40)),
        "convb": np.ascontiguousarray(inputs["conv_b"][L].reshape(8, 128).T),
        "gateb": np.ascontiguousarray(np.broadcast_to(inputs["gate_b"][L][gord][None, :], (128, 16))),
        "mhnw": np.ascontiguousarray(np.broadcast_to(inputs["mhn_w"][L][None, :], (128, 512))),
        "sink": np.ascontiguousarray(np.broadcast_to(inputs["sink"][L][None, :], (128, 8))),
        "fnw": np.ascontiguousarray(np.broadcast_to(inputs["final_norm_w"][None, :], (128, D))),
    }
    return d


_CACHE = {}


DBG = False
LAST = {}


def get_prog(NT, depth):
    key = (NT, depth, DBG)
    if key not in _CACHE:
        _CACHE[key] = build(NT, depth, dbg=DBG)[0]
    return _CACHE[key]


def kernel(x, norm_w, w_in, conv_w, conv_b, gate_b, mhn_w, sink, rel_bias, w_out, final_norm_w):
    inputs = dict(x=np.asarray(x), norm_w=np.asarray(norm_w), w_in=np.asarray(w_in), conv_w=np.asarray(conv_w),
                  conv_b=np.asarray(conv_b), gate_b=np.asarray(gate_b), mhn_w=np.asarray(mhn_w), sink=np.asarray(sink),
                  rel_bias=np.asarray(rel_bias), w_out=np.asarray(w_out), final_norm_w=np.asarray(final_norm_w))
    B, S, _ = inputs["x"].shape
    half = S // 2
    NT = half // 128
    depth = inputs["w_in"].shape[0]
    cst, selc = make_consts()
    bias = [make_bias_tables(inputs["rel_bias"], f) for f in (False, True)]
    lay = []
    for f in (False, True):
        per = [layer_inputs(inputs, L, f) for L in range(depth)]
        d = {k: np.ascontiguousarray(np.stack([p[k] for p in per])) for k in per[0] if k != "fnw"}
        d["fnw"] = per[0]["fnw"]
        lay.append(d)
    xs = []
    for c in range(8):
        b, hf = c // 2, c % 2
        seg = inputs["x"][b, hf * half:(hf + 1) * half]
        xs.append(np.ascontiguousarray(seg[::-1] if hf else seg))
    maps = []
    for c in range(8):
        m = dict(lay[c % 2])
        msel = np.zeros((128, 2), np.float32)
        msel[:, 1 - (c % 2)] = 1.0
        m.update({"x": xs[c], "xh": np.ascontiguousarray(xs[c ^ 1][(NT - 1) * 128:]), "biasT": bias[c % 2],
                  "consts": cst, "selc": selc, "msel": msel})
        maps.append(m)
    nc = get_prog(NT, depth)
    res = run_bass_kernel_spmd(nc, maps, core_ids=list(range(8)))
    LAST["res"] = res
    out = np.empty((B, S, D), np.float32)
    for c in range(8):
        b, hf = c // 2, c % 2
        o = np.asarray(res.results[c]["x_out"])
        out[b, hf * half:(hf + 1) * half] = o[::-1] if hf else o
    return out
```
